# Optimizing a Trainium2 kernel written in Bass

```python
import jax, jax.numpy as jnp
from jax import lax
import numpy as np

D_MODEL = 1024
BATCH = 4
SEQ = 4096
DEPTH = 1
DEC_BATCH = 128
DEC_SEQ = 1
PAST_LEN = 16384
PAGE_SIZE = 128

HEAD_DIM = 64
ATTN_Q_HEADS = 8
ATTN_KV_HEADS = 2
ATTN_GROUP = ATTN_Q_HEADS // ATTN_KV_HEADS
WINDOW = 128
HG_HEADS = 4
HG_DK = 128
HG_DV = 128
HG_CHUNK = 64
FFN_DIM = 2816
EPS = 1e-6

ATTN_Q_W = ATTN_Q_HEADS * HEAD_DIM
ATTN_KV_W = ATTN_KV_HEADS * HEAD_DIM
HG_K_W = HG_HEADS * HG_DK
HG_V_W = HG_HEADS * HG_DV
SPLIT_SIZES = (ATTN_Q_W, ATTN_KV_W, ATTN_KV_W, HG_K_W, HG_K_W, HG_V_W, HG_V_W, D_MODEL, D_MODEL)
IN_WIDTH = sum(SPLIT_SIZES)

kernel_name = "hybrid_swa_hgrn2_macaron_step"


def rmsnorm(x, g):
    xf = x.astype(jnp.float32)
    y = xf * lax.rsqrt(jnp.mean(xf * xf, axis=-1, keepdims=True) + EPS)
    return (y * g.astype(jnp.float32)).astype(x.dtype)


def split_cols(z):
    out, start = [], 0
    for w in SPLIT_SIZES:
        out.append(z[..., start:start + w])
        start += w
    return out


def ffn_half(x, pre, post, wg, wu, wd):
    h = rmsnorm(x, pre)
    u = jax.nn.silu(h @ wg) * (h @ wu)
    return x + 0.5 * rmsnorm(u @ wd, post)


def sink_attend(s, valid, v, sinks, eq):
    s = jnp.where(valid, s, -jnp.inf)
    sk = sinks.astype(jnp.float32).reshape(ATTN_KV_HEADS, ATTN_GROUP, 1, 1)
    m = jnp.maximum(jnp.max(s, axis=-1, keepdims=True), sk)
    p = jnp.exp(s - m)
    p = p / (jnp.sum(p, axis=-1, keepdims=True) + jnp.exp(sk - m))
    return jnp.einsum(eq, p.astype(v.dtype), v)


def swa_prompt(q, k, v, sinks):
    B, T = q.shape[:2]
    nb = T // WINDOW
    qb = q.reshape(B, nb, WINDOW, ATTN_KV_HEADS, ATTN_GROUP, HEAD_DIM)

    def band(t):
        tp = jnp.concatenate([jnp.zeros_like(t[:, :WINDOW]), t], axis=1)
        tp = tp.reshape(B, nb + 1, WINDOW, ATTN_KV_HEADS, HEAD_DIM)
        return jnp.concatenate([tp[:, :-1], tp[:, 1:]], axis=2)

    kb, vb = band(k), band(v)
    s = jnp.einsum('bnqhgd,bnkhd->bnhgqk', qb, kb,
                   preferred_element_type=jnp.float32) * (HEAD_DIM ** -0.5)
    i = jnp.arange(WINDOW)[:, None]
    j = jnp.arange(2 * WINDOW)[None, :]
    n = jnp.arange(nb)[:, None, None]
    rel = i + WINDOW - j
    valid = (rel >= 0) & (rel < WINDOW) & (n * WINDOW - WINDOW + j >= 0)
    o = sink_attend(s, valid[None, :, None, None], vb, sinks, 'bnhgqk,bnkhd->bnqhgd')
    return o.reshape(B, T, ATTN_Q_W), k[:, -WINDOW:], v[:, -WINDOW:]


def swa_sample(q, k, v, ck, cv, sinks):
    B, T = q.shape[:2]
    kk = jnp.concatenate([ck, k], axis=1)
    vv = jnp.concatenate([cv, v], axis=1)
    qg = q.reshape(B, T, ATTN_KV_HEADS, ATTN_GROUP, HEAD_DIM)
    s = jnp.einsum('bqhgd,bkhd->bhgqk', qg, kk,
                   preferred_element_type=jnp.float32) * (HEAD_DIM ** -0.5)
    qpos = jnp.arange(T)[:, None]
    kpos = jnp.arange(WINDOW + T)[None, :] - WINDOW
    valid = (kpos <= qpos) & (kpos > qpos - WINDOW)
    o = sink_attend(s, valid, vv, sinks, 'bhgqk,bkhd->bqhgd')
    return o.reshape(B, T, ATTN_Q_W), kk[:, -WINDOW:], vv[:, -WINDOW:]


def hgrn_chunked(q, k, v, logf, s0):
    B, T, H, DK = q.shape
    C = min(HG_CHUNK, T)
    n = -(-T // C)
    pad = n * C - T

    def prep(t):
        if pad:
            t = jnp.pad(t, ((0, 0), (0, pad), (0, 0), (0, 0)))
        return t.reshape(B, n, C, H, t.shape[-1]).transpose(1, 0, 3, 2, 4)

    tri = jnp.tril(jnp.ones((C, C), dtype=bool))[:, :, None]

    def step(S, inp):
        qc, kc, vc, gc = inp
        b = jnp.cumsum(gc, axis=2)
        o_inter = jnp.einsum('bhtk,bhkv->bhtv', qc * jnp.exp(b), S)
        diff = b[:, :, :, None, :] - b[:, :, None, :, :]
        decay = jnp.exp(jnp.where(tri, diff, -jnp.inf))
        A = jnp.einsum('bhtk,bhsk,bhtsk->bhts', qc, kc, decay)
        o = o_inter + jnp.einsum('bhts,bhsv->bhtv', A, vc)
        b_last = b[:, :, -1:, :]
        S_new = jnp.exp(b_last[:, :, 0, :])[..., None] * S + jnp.einsum(
            'bhsk,bhsv->bhkv', kc * jnp.exp(b_last - b), vc)
        return S_new, o

    S_fin, o = lax.scan(step, s0, (prep(q), prep(k), prep(v), prep(logf)))
    o = o.transpose(1, 0, 3, 2, 4).reshape(B, n * C, H, v.shape[-1])[:, :T]
    return o, S_fin


def token_mix(h, w_in, sinks, lb, hg_norm, w_attn_out, w_hgrn_out, w_out, attn_fn, s0):
    B, T, _ = h.shape
    q_a, k_a, v_a, q_h, f_h, i_h, g_h, gate_a, gate_h = split_cols(h @ w_in)
    o_a, nk, nv = attn_fn(q_a.reshape(B, T, ATTN_Q_HEADS, HEAD_DIM),
                          k_a.reshape(B, T, ATTN_KV_HEADS, HEAD_DIM),
                          v_a.reshape(B, T, ATTN_KV_HEADS, HEAD_DIM), sinks)
    fpre = f_h.astype(jnp.float32).reshape(B, T, HG_HEADS, HG_DK)
    lbh = lb.astype(jnp.float32).reshape(HG_HEADS, HG_DK)
    logf = jnp.log(lbh + (1.0 - lbh) * jax.nn.sigmoid(fpre))
    kh = (1.0 - lbh) * jax.nn.sigmoid(-fpre)
    qh = jax.nn.silu(q_h.astype(jnp.float32)).reshape(B, T, HG_HEADS, HG_DK)
    vh = i_h.astype(jnp.float32).reshape(B, T, HG_HEADS, HG_DV)
    o_h, s_new = hgrn_chunked(qh, kh, vh, logf, s0.astype(jnp.float32))
    o_h = rmsnorm(o_h, hg_norm) * jax.nn.silu(
        g_h.astype(jnp.float32).reshape(B, T, HG_HEADS, HG_DV))
    o_h = o_h.reshape(B, T, HG_V_W).astype(h.dtype)
    m = jax.nn.sigmoid(gate_a) * (o_a @ w_attn_out) + jax.nn.sigmoid(gate_h) * (o_h @ w_hgrn_out)
    return m @ w_out, nk, nv, s_new.astype(s0.dtype)


def setup_inputs(seed: int = 0) -> dict:
    key = jax.random.key(seed)
    ks = jax.random.split(key, 32)
    f32 = jnp.float32

    def nrm(k, shape, scale):
        return jax.random.normal(k, shape, f32) * scale

    def gain(k, shape):
        return 1.0 + 0.05 * jax.random.normal(k, shape, f32)

    D, F = D_MODEL, FFN_DIM
    return {
        "x_prompt": nrm(ks[0], (BATCH, SEQ, D), 1.0),
        "x_sample": nrm(ks[1], (DEC_BATCH, DEC_SEQ, D), 1.0),
        "cache_k": nrm(ks[2], (DEPTH, DEC_BATCH, WINDOW, ATTN_KV_HEADS, HEAD_DIM), 1.0),
        "cache_v": nrm(ks[3], (DEPTH, DEC_BATCH, WINDOW, ATTN_KV_HEADS, HEAD_DIM), 1.0),
        "state_hgrn": nrm(ks[4], (DEPTH, DEC_BATCH, HG_HEADS, HG_DK, HG_DV), 0.5),
        "norm_ffn1_pre": gain(ks[5], (DEPTH, D)),
        "norm_ffn1_post": gain(ks[6], (DEPTH, D)),
        "w_ffn1_gate": nrm(ks[7], (DEPTH, D, F), D ** -0.5),
        "w_ffn1_up": nrm(ks[8], (DEPTH, D, F), D ** -0.5),
        "w_ffn1_down": nrm(ks[9], (DEPTH, F, D), F ** -0.5),
        "norm_mix_pre": gain(ks[10], (DEPTH, D)),
        "norm_mix_post": gain(ks[11], (DEPTH, D)),
        "w_in": nrm(ks[12], (DEPTH, D, IN_WIDTH), D ** -0.5),
        "attn_sinks": nrm(ks[13], (DEPTH, ATTN_Q_HEADS), 0.5),
        "hgrn_lb_logits": nrm(ks[14], (DEPTH + 1, HG_K_W), 0.1),
        "hgrn_norm": gain(ks[15], (DEPTH, HG_DV)),
        "w_attn_out": nrm(ks[16], (DEPTH, ATTN_Q_W, D), ATTN_Q_W ** -0.5),
        "w_hgrn_out": nrm(ks[17], (DEPTH, HG_V_W, D), HG_V_W ** -0.5),
        "w_out": nrm(ks[18], (DEPTH, D, D), D ** -0.5),
        "norm_ffn2_pre": gain(ks[19], (DEPTH, D)),
        "norm_ffn2_post": gain(ks[20], (DEPTH, D)),
        "w_ffn2_gate": nrm(ks[21], (DEPTH, D, F), D ** -0.5),
        "w_ffn2_up": nrm(ks[22], (DEPTH, D, F), D ** -0.5),
        "w_ffn2_down": nrm(ks[23], (DEPTH, F, D), F ** -0.5),
    }


def reference(x_prompt, x_sample, cache_k, cache_v, state_hgrn,
              norm_ffn1_pre, norm_ffn1_post, w_ffn1_gate, w_ffn1_up, w_ffn1_down,
              norm_mix_pre, norm_mix_post, w_in, attn_sinks, hgrn_lb_logits, hgrn_norm,
              w_attn_out, w_hgrn_out, w_out,
              norm_ffn2_pre, norm_ffn2_post, w_ffn2_gate, w_ffn2_up, w_ffn2_down):
    lb_all = jnp.cumsum(jax.nn.softmax(hgrn_lb_logits.astype(jnp.float32), axis=0), axis=0)
    xp, xs = x_prompt, x_sample
    kp_l, vp_l, sp_l, ks_l, vs_l, ss_l = [], [], [], [], [], []
    for l in range(DEPTH):
        xp = ffn_half(xp, norm_ffn1_pre[l], norm_ffn1_post[l], w_ffn1_gate[l], w_ffn1_up[l], w_ffn1_down[l])
        xs = ffn_half(xs, norm_ffn1_pre[l], norm_ffn1_post[l], w_ffn1_gate[l], w_ffn1_up[l], w_ffn1_down[l])

        s0p = jnp.zeros((xp.shape[0], HG_HEADS, HG_DK, HG_DV), xp.dtype)
        mp, kp, vp, sp = token_mix(rmsnorm(xp, norm_mix_pre[l]), w_in[l], attn_sinks[l], lb_all[l],
                                   hgrn_norm[l], w_attn_out[l], w_hgrn_out[l], w_out[l],
                                   swa_prompt, s0p)
        ck, cv = cache_k[l], cache_v[l]
        samp_attn = lambda q, k, v, sk: swa_sample(q, k, v, ck, cv, sk)
        ms, ksn, vsn, ssn = token_mix(rmsnorm(xs, norm_mix_pre[l]), w_in[l], attn_sinks[l], lb_all[l],
                                      hgrn_norm[l], w_attn_out[l], w_hgrn_out[l], w_out[l],
                                      samp_attn, state_hgrn[l])
        xp = xp + rmsnorm(mp, norm_mix_post[l])
        xs = xs + rmsnorm(ms, norm_mix_post[l])

        xp = ffn_half(xp, norm_ffn2_pre[l], norm_ffn2_post[l], w_ffn2_gate[l], w_ffn2_up[l], w_ffn2_down[l])
        xs = ffn_half(xs, norm_ffn2_pre[l], norm_ffn2_post[l], w_ffn2_gate[l], w_ffn2_up[l], w_ffn2_down[l])
        kp_l.append(kp); vp_l.append(vp); sp_l.append(sp)
        ks_l.append(ksn); vs_l.append(vsn); ss_l.append(ssn)

    new_k_prompt = jnp.stack(kp_l)
    new_v_prompt = jnp.stack(vp_l)
    new_hgrn_prompt = jnp.stack(sp_l)
    new_k_sample = jnp.stack(ks_l)
    new_v_sample = jnp.stack(vs_l)
    new_hgrn_sample = jnp.stack(ss_l)
    return (xp, xs, new_k_prompt, new_v_prompt, new_hgrn_prompt, new_k_sample, new_v_sample, new_hgrn_sample)
```

```python
import os
from contextlib import ExitStack

import numpy as np
import concourse.bass as bass
import concourse.mybir as mybir
from concourse.bass_utils import run_bass_kernel_spmd

F32 = mybir.dt.float32
BF16 = mybir.dt.bfloat16
AF = mybir.ActivationFunctionType
ALU = mybir.AluOpType
AX = mybir.AxisListType

D = 1024
FF = 2816
NSEG = int(os.environ.get("K_NSEG", "4"))
TS = 512
NT = 4
NSAMP = 16
EPS = 1e-6
FEAT = "skq2"
NEG = -30000.0
GROUPS = [(0, 4), (4, 4), (8, 4), (12, 4), (16, 3), (19, 3)]


_DECL = []


class Buf:
    def __init__(self, ap, keys):
        self.ap = ap
        self.keys = list(keys)


class Sched:
    def __init__(self, nc, es):
        self.nc, self.es = nc, es
        self.eng = dict(pe=nc.tensor, act=nc.scalar, dve=nc.vector, pool=nc.gpsimd, sp=nc.sync)
        self.csem = {e: es.enter_context(nc.semaphore("c_" + e)) for e in ("pe", "act", "dve", "pool")}
        self.cnt = {e: 0 for e in self.csem}
        self.waited = {e: {} for e in self.eng}
        self.res = {}
        self.dsem = {}
        self.sems = {}
        self.phase = "init"
        self.pmap = {}

    @staticmethod
    def _keys(bufs):
        out = []
        for b in bufs:
            if isinstance(b, Buf):
                out.extend(b.keys)
            elif isinstance(b, list):
                out.extend(Sched._keys(b))
            else:
                out.append(b)
        return out

    def _deps(self, rk, wk):
        toks = []
        for k in rk:
            st = self.res.get(k)
            if st and st[0]:
                toks.append(st[0])
        for k in wk:
            st = self.res.get(k)
            if st:
                if st[0]:
                    toks.append(st[0])
                toks.extend(st[1].values())
        return toks

    def _waits(self, e, toks):
        best = {}
        for (sid, val, src) in toks:
            if src == "pe" and e == "pe":
                continue
            if val > best.get(sid, 0):
                best[sid] = val
        for sid, val in best.items():
            if self.waited[e].get(sid, 0) >= val:
                continue
            self.eng[e].wait_ge(self.sems[sid], val)
            self.waited[e][sid] = val

    def _record(self, tok, rk, wk):
        wk = set(wk)
        for k in wk:
            self.res[k] = [tok, {}]
        for k in rk:
            if k in wk:
                continue
            st = self.res.setdefault(k, [None, {}])
            st[1][tok[0]] = tok

    def op(self, e, fn, reads=(), writes=()):
        rk, wk = self._keys(reads), self._keys(writes)
        self._waits(e, self._deps(rk, wk))
        ins = fn(self.eng[e])
        try:
            self.pmap[ins.ins.name] = self.phase
        except Exception:
            pass
        self.cnt[e] += 1
        ins.then_inc(self.csem[e], 1)
        sid = "c_" + e
        self.sems[sid] = self.csem[e]
        self._record((sid, self.cnt[e], e), rk, wk)

    def dma(self, q, pairs, reads, writes, sem):
        rk, wk = self._keys(reads), self._keys(writes)
        self._waits(q, self._deps(rk, wk))
        if sem not in self.dsem:
            self.dsem[sem] = [self.es.enter_context(self.nc.semaphore("d_" + sem)), 0]
            self.sems["d_" + sem] = self.dsem[sem][0]
        s = self.dsem[sem]
        for (o, i) in pairs:
            self.eng[q].dma_start(out=o, in_=i).then_inc(s[0], 16)
            s[1] += 16
        self._record(("d_" + sem, s[1], "dma"), rk, wk)

    def collective(self, kind, groups, src, dst, reads, writes):
        rk, wk = self._keys(reads), self._keys(writes)
        self._waits("pool", self._deps(rk, wk))
        sem = self.es.enter_context(self.nc.semaphore("cc_sem"))
        self.sems["cc_sem"] = sem
        self.nc.gpsimd.collective_compute(kind, mybir.AluOpType.bypass, replica_groups=groups, ins=[src],
                                          outs=[dst]).then_inc(sem)
        self._record(("cc_sem", 1, "cc"), rk, wk)

    def finish(self):
        for name, (sem, cnt) in self.dsem.items():
            if cnt:
                self.nc.sync.wait_ge(sem, cnt)


def build(stage=99):
    nc = bass.Bass("TRN2", target_bir_lowering=False)
    es = ExitStack()
    es.enter_context(nc.allow_non_contiguous_dma(reason="small strided constant loads"))
    S = Sched(nc, es)

    _DECL.clear()

    def din(name, shape):
        _DECL.append(name)
        return nc.dram_tensor(name, list(shape), F32, kind="ExternalInput").ap()

    def dout(name, shape):
        return nc.dram_tensor(name, list(shape), F32, kind="ExternalOutput").ap()

    xp = din("xp", [NSEG * TS, D])
    xs = din("xs", [NSAMP, D])
    ck = din("ck", [NSAMP, 128, 128])
    cv = din("cv", [NSAMP, 128, 128])
    st = din("st", [NSAMP, 4, 128, 128])
    w = {}
    for n_, sh in [("w1g", [D, FF]), ("w1u", [D, FF]), ("w1d", [FF, D]), ("win", [D, 4864]),
                   ("wao", [512, D]), ("who", [512, D]), ("wout", [D, D]),
                   ("w2g", [D, FF]), ("w2u", [D, FF]), ("w2d", [FF, D])]:
        w[n_] = din(n_, sh)
    vec = din("vec", [11, D])
    cmat = din("cmat", [128, 800])
    VROW = {"n1pre": 0, "n1post": 1, "nmpre": 2, "nmpost": 3, "n2pre": 4, "n2post": 5, "sinks": 6, "lbl0": 7,
            "lbl1": 8, "hgn": 9}
    for k_, r_ in VROW.items():
        w[k_] = vec[r_:r_ + 1, :]
    w["c_ident"] = cmat[:, 0:128]
    w["c_tri"] = cmat[:, 128:256]
    w["c_mask"] = cmat[:, 256:512]
    w["c_mask0"] = cmat[:, 512:768]
    w["c_id16"] = cmat[0:16, 768:784]
    w["c_dm"] = cmat[0:64, 784:800]
    yp = dout("yp", [NSEG * TS, D])
    ys = dout("ys", [NSAMP, D])
    nkp = dout("nkp", [128, 128])
    nvp = dout("nvp", [128, 128])
    nhp = dout("nhp", [4, 128, 128])
    nks = dout("nks", [NSAMP, 128, 128])
    nvs = dout("nvs", [NSAMP, 128, 128])
    nhs = dout("nhs", [NSAMP, 4, 128, 128])
    x1s = nc.dram_tensor("x1s", [NSEG * TS + NSAMP, D], F32).ap()
    xsend = nc.dram_tensor("xsend", [128, 1024], F32).ap()
    spH = nc.dram_tensor("spH", [NSEG, 128, 8 * (TS + NSAMP)], BF16).ap()
    spP = nc.dram_tensor("spP", [NSEG, 128, 2048], F32).ap()
    spK = nc.dram_tensor("spK", [NSEG, 128, 4096], BF16).ap()
    spK2 = nc.dram_tensor("spK2", [NSEG, 128, 1024], BF16).ap()
    spPt = nc.dram_tensor("spPt", [NSEG, 128, 4 * NT], F32).ap()
    xrecv = nc.dram_tensor("xrecv", [256, 1024], F32).ap()

    def sb(name, shape, dt=F32):
        return es.enter_context(nc.sbuf_tensor(name, list(shape), dt))

    NTT = NT + 1
    NC = TS + NSAMP
    GR = 512
    R_t = sb("R", [128, NTT, D])
    hT_t = sb("hT", [128, 8, NC], BF16)
    AR_t = sb("arena", [128, 38912], BF16)
    oaT_t = sb("oaT", [128, 4, NC], BF16)
    ohT_t = sb("ohT", [128, 4, NC], BF16)
    R = [Buf(R_t[:, i, :], [("R", i)]) for i in range(NTT)]
    hT = Buf(hT_t, ["hT"])
    oaT = Buf(oaT_t, ["oaT"])
    ohT = Buf(ohT_t, ["ohT"])

    def akeys(off, n):
        return [("AR", g) for g in range(off // GR, (off + n - 1) // GR + 1)]

    def arena(off, n, shape_str=None, dt=BF16, **kw):
        ap = AR_t[:, off:off + n]
        if dt == F32:
            ap = ap.bitcast(F32)
        if shape_str:
            ap = ap.rearrange(shape_str, **kw)
        b = Buf(ap, akeys(off, n))
        b.off = off
        return b

    Y_b = arena(0, 2 * NTT * D, "p (i n) -> p i n", dt=F32, i=NTT)
    Y_t = Y_b.ap

    def ykeys(i, half=None):
        if half is None:
            return akeys(i * 2 * D, 2 * D)
        return akeys(i * 2 * D + half * D, D)
    UT0 = 22528
    FSLOT = [10240, 26624]
    uT = [arena(UT0 + i * 2048, 2048, "p (j n) -> p j n", j=4) for i in range(2)]
    TA = 16384
    qT = arena(TA, 4096, "p (g n) -> p g n", g=8)
    smk = arena(TA + 4096, 4096, "p (h k) -> p h k", dt=F32, h=8)
    sk = arena(TA + 4096, 4224, "p (h k) -> p h k", dt=F32, h=8)
    pbf = arena(TA + 8320, 2176, "p (h k) -> p h k", h=8)
    pTs = arena(TA + 10496, 2048, "p (b n) -> p b n", b=16)
    KTs = arena(TA + 12544, 2048)
    Ks = arena(TA + 14592, 2048, "p (b n) -> p b n", b=16)
    Vsj = [arena(TA + 16640 + j * 1024, 1024, "p (b n) -> p b n", b=16) for j in range(2)]
    pbf2 = arena(TA + 18688, 2176, "p (h k) -> p h k", h=8)
    pbfs = [pbf, pbf2]

    def skk(b4):
        return akeys(sk.off + b4 * 1056, 1056)
    THG = 16384
    Pm = [arena(THG + h * 1024, 1024, dt=F32) for h in range(4)]
    qeT = [arena(THG + 4096 + h * 512, 512) for h in range(4)]
    keT = [arena(THG + 6144 + h * 512, 512) for h in range(4)]
    kdT = [arena(THG + 8192 + h * 512, 512) for h in range(4)]
    smk2 = arena(THG + 10240, 4096, "p (h k) -> p h k", dt=F32, h=8)
    Ssg = arena(THG + 14336, 4096, "p (h b v) -> p h b v", dt=F32, h=4, b=4)
    Ssg2 = arena(THG + 4096, 4096, "p (h b v) -> p h b v", dt=F32, h=4, b=4)
    vexp = [arena(THG + 18432 + i * 2048, 2048, "p (b v) -> p b v", b=16) for i in range(2)] + \
           [arena(THG + i * 2048, 2048, "p (b v) -> p b v", b=16) for i in range(2)]
    mT0 = arena(6144, 8 * NC, "p (m c) -> p m c", m=8)
    mT1 = arena(6144, 8 * TS, "p (m c) -> p m c", m=8)

    def smkk(buf, b4):
        return akeys(buf.off + b4 * 2048, 2048)

    ident = Buf(sb("ident", [128, 128], BF16), ["ident"])
    tri = Buf(sb("tri", [128, 128]), ["tri"])
    maskN = Buf(sb("maskN", [128, 256]), ["maskN"])
    mask0 = Buf(sb("mask0", [128, 256]), ["mask0"])
    id16 = Buf(sb("id16", [16, 16]), ["id16"])
    dm = Buf(sb("dm", [64, 16]), ["dm"])
    gpre = {k: Buf(sb("g_" + k, [128, 8]), ["g_" + k]) for k in ("n1pre", "nmpre", "n2pre")}
    gpost = {k: Buf(sb("g_" + k, [128, D]), ["g_" + k]) for k in ("n1post", "nmpost", "n2post")}
    hgnb = Buf(sb("hgnb", [128, 128]), ["hgnb"])
    sinkb = Buf(sb("sinkb", [128, 8]), ["sinkb"])
    sinkc = Buf(sb("sinkc", [64, 2]), ["sinkc"])
    lbt = Buf(sb("lbt", [128, 2, 4]), ["lbt"])
    cc = Buf(sb("cc", [128, 4]), ["cc"])
    ccn = Buf(sb("ccn", [128, 4]), ["ccn"])
    cc1 = Buf(sb("cc1", [128, 4]), ["cc1"])
    lbr = Buf(sb("lbr", [16, 2, 512]), ["lbr"])
    ccr = Buf(sb("ccr", [16, 512]), ["ccr"])
    ccrn = Buf(sb("ccrn", [16, 512]), ["ccrn"])
    zer = Buf(sb("zer", [128, 64]), ["zer"])
    d1z = Buf(sb("d1z", [128, 512]), ["d1z"])
    nhalf = Buf(sb("nhalf", [128, 16]), ["nhalf"])
    ss = Buf(sb("ss", [128, 16]), ["ss"])
    rstd = Buf(sb("rstd", [128, 16]), ["rstd"])
    junk = Buf(sb("junk", [128, D], BF16), ["junk"])
    hb = [Buf(sb("hb%d" % i, [128, D], BF16), ["hb%d" % i]) for i in range(2)]
    tmpA = [Buf(sb("tmpA%d" % i, [128, 512]), ["tmpA%d" % i]) for i in range(2)]
    tmpB = [Buf(sb("tmpB%d" % i, [128, 512]), ["tmpB%d" % i]) for i in range(2)]
    tmpC = [Buf(sb("tmpC%d" % i, [128, 512]), ["tmpC%d" % i]) for i in range(2)]
    tmpD = [Buf(sb("tmpD%d" % i, [128, D]), ["tmpD%d" % i]) for i in range(2)]
    kT = Buf(sb("kT", [64, 2, 128 + NC], BF16), ["kT"])
    vtok = [Buf(sb("vtok%d" % i, [128, 128], BF16), [("vtok", i)]) for i in range(NT + 1)]
    kvf = Buf(sb("kvf", [128, 256]), ["kvf"])
    st8 = {k: Buf(sb("st8" + k, [128, 8]), ["st8" + k]) for k in ("mx", "nmx", "sum", "es", "rinv")}
    st8b = {k: Buf(sb("st8b" + k, [128, 8]), ["st8b" + k]) for k in ("mx", "nmx", "sum", "es", "rinv")}
    st8s = [st8, st8b]
    obf = Buf(sb("obf", [128, 512], BF16), ["obf"])
    Sf = Buf(sb("Sf", [128, 4, 128]), ["Sf"])
    Sb16 = Buf(sb("Sb16", [128, 4, 128], BF16), ["Sb16"])
    Sb16m = Buf(sb("Sb16m", [128, 4, 128], BF16), ["Sb16m"])
    SB = [Sb16, Sb16m]
    par = [0]
    Pt = Buf(sb("Pt", [128, 4, NT]), ["Pt"])
    kd2c0 = Buf(sb("kd2c0", [128, 4, NT * 64], BF16), ["kd2c0"])
    qe2c1 = Buf(sb("qe2c1", [128, 4, NT * 64], BF16), ["qe2c1"])
    wpre = Buf(sb("wpre", [128, 8, 1024], BF16), ["wpre"])
    kdtok = Buf(sb("kdtok", [128, 4, 128], BF16), ["kdtok"])
    vh = Buf(sb("vh", [128, 512], BF16), ["vh"])
    sgg = Buf(sb("sgg", [128, 512]), ["sgg"])
    ATs = Buf(sb("ATs", [128, 4, 128], BF16), ["ATs"])
    ohn = Buf(sb("ohn", [128, 512], BF16), ["ohn"])
    ssh = Buf(sb("ssh", [128, 4]), ["ssh"])
    rsh = Buf(sb("rsh", [128, 4]), ["rsh"])
    qs = Buf(sb("qs", [64, 8, 16], BF16), ["qs"])
    s_s = Buf(sb("s_s", [64, 128]), ["s_s"])
    p_s = Buf(sb("p_s", [64, 128], BF16), ["p_s"])
    pT_s = Buf(sb("pT_s", [128, 64], BF16), ["pT_s"])
    o_s = Buf(sb("o_s", [64, 64]), ["o_s"])
    o_sb = Buf(sb("o_sb", [64, 64], BF16), ["o_sb"])
    oTs = Buf(sb("oTs", [64, 8, 16], BF16), ["oTs"])
    st1 = {k: Buf(sb("st1" + k, [64, 1]), ["st1" + k]) for k in ("mx", "nmx", "sum", "es", "rinv")}
    khtok = Buf(sb("khtok", [16, 512], BF16), ["khtok"])
    fvs = Buf(sb("fvs", [128, 4, 16]), ["fvs"])
    qss = Buf(sb("qss", [128, 4, 16]), ["qss"])
    o_hs = Buf(sb("o_hs", [16, 512]), ["o_hs"])

    PS = [Buf(es.enter_context(nc.psum_tensor("ps%d" % i, [128, 512], F32)), [("ps", i)]) for i in range(8)]

    def psb(i):
        return PS[i].ap[:].bitcast(BF16)

    def act(out, in_, func, reads, writes, **kw):
        S.op("act", lambda e: e.activation(out=out, in_=in_, func=func, **kw), reads, writes)

    def tt(out, a, b, op, reads, writes, eng="dve"):
        S.op(eng, lambda e: e.tensor_tensor(out=out, in0=a, in1=b, op=op), reads, writes)

    def ts(out, a, s1, s2, op0, op1, reads, writes, eng="dve"):
        if s2 is None:
            S.op(eng, lambda e: e.tensor_scalar(out=out, in0=a, scalar1=s1, scalar2=None, op0=op0), reads, writes)
        else:
            S.op(eng, lambda e: e.tensor_scalar(out=out, in0=a, scalar1=s1, scalar2=s2, op0=op0, op1=op1),
                 reads, writes)

    def stt(out, a, s, b, op0, op1, reads, writes):
        S.op("dve", lambda e: e.scalar_tensor_tensor(out=out, in0=a, scalar=s, in1=b, op0=op0, op1=op1),
             reads, writes)

    def cp(out, in_, reads, writes, eng="dve"):
        S.op(eng, lambda e: e.tensor_copy(out=out, in_=in_), reads, writes)

    def mm(out, pairs, reads, writes):
        def f(e):
            n = len(pairs)
            for i, (l, r) in enumerate(pairs):
                ins = e.matmul(out, lhsT=l, rhs=r, start=(i == 0), stop=(i == n - 1))
            return ins
        S.op("pe", f, reads, writes)

    def rsqrt(out, in_, scale, eps, reads, writes, tmp, key="rsq_tmp"):
        ts(tmp, in_, scale, eps, ALU.mult, ALU.add, reads, [key])
        S.op("pool", lambda e: e.tensor_tensor(out=out, in0=tmp, in1=nhalf.ap[0:tmp.shape[0], 0:tmp.shape[1]],
                                               op=ALU.pow), [key, nhalf], writes)

    rsq_tmp = sb("rsq_tmp", [128, 16])
    rsq_tmpn = sb("rsq_tmpn", [128, 8])
    rstdn = sb("rstdn", [128, 8])

    def early_ffn_load():
        h0, G = GROUPS[0]
        o = 26624
        a = arena(o, 4096, "p (k n) -> p k n", k=8)
        b = arena(o + 4096, 4096, "p (k n) -> p k n", k=8)
        c = arena(o + 8192, 4096, "p (k n) -> p k n", k=4)
        S.dma("pool", [(a.ap[:, :, 0:G * 128], w["w1g"].rearrange("(k p) n -> p k n", p=128)[:, :, h0 * 128:(h0 + G) * 128]),
                       (b.ap[:, :, 0:G * 128], w["w1u"].rearrange("(k p) n -> p k n", p=128)[:, :, h0 * 128:(h0 + G) * 128]),
                       (c.ap[:, 0:G, :], w["w1d"].rearrange("(k p) n -> p k n", p=128)[:, h0:h0 + G, :])], [], [a, b, c], "wf1")

    def ld(q, buf, src, sem):
        S.dma(q, [(buf.ap[:] if not isinstance(buf.ap, bass.AP) else buf.ap, src)], [], [buf], sem)

    if True:
        ld("pool", ident, w["c_ident"], "c0")
        for (i_, tp_, c0_) in [(i, 128, i * 128) for i in range(NT)]:
            S.dma("sp", [(R[i_].ap[:, :], xp[c0_: c0_ + 128, :])], [], [R[i_]], "ldx%d" % (i_ % 4))
        early_ffn_load()
        S.op("dve", lambda e: e.memset(R[NT].ap[:, :], 0.0), [], [R[NT]])
        S.dma("sp", [(R[NT].ap[0:NSAMP, :], xs[:, :])], [], [R[NT]], "ldxs")
        S.dma("sp", [(gpre["n1pre"].ap[:], w["n1pre"].rearrange("o (k p) -> p (o k)", p=128))], [], [gpre["n1pre"]], "c60")
        ld("sp", tri, w["c_tri"], "c1")
        ld("sp", maskN, w["c_mask"], "c2")
        ld("sp", mask0, w["c_mask0"], "c3")
        ld("sp", id16, w["c_id16"], "c4")
        ld("sp", dm, w["c_dm"], "c5")
        for i_, k in enumerate(("n1pre", "nmpre", "n2pre")):
            if i_ > 0:
                S.dma("sp", [(gpre[k].ap[:], w[k].rearrange("o (k p) -> p (o k)", p=128))], [], [gpre[k]], "c6%d" % i_)
        for i_, k in enumerate(("n1post", "nmpost", "n2post")):
            S.dma("sp", [(gpost[k].ap[:], w[k].broadcast_to([128, D]))], [], [gpost[k]], "c7%d" % i_)
        S.dma("sp", [(hgnb.ap[:], w["hgn"][:, 0:128].broadcast_to([128, 128]))], [], [hgnb], "c8")
        S.dma("sp", [(sinkb.ap[:], w["sinks"][:, 0:8].broadcast_to([128, 8]))], [], [sinkb], "c9")
        S.dma("sp", [(sinkc.ap[g * 16:(g + 1) * 16, j:j + 1],
                      w["sinks"][0:1, 4 * j + g:4 * j + g + 1].broadcast_to([16, 1]))
                     for j in range(2) for g in range(4)], [], [sinkc], "ca")
        S.dma("sp", [(lbt.ap[:, r_, :], w["lbl%d" % r_][:, 0:512].rearrange("o (h p) -> p (o h)", p=128))
                     for r_ in range(2)], [], [lbt], "cb")
        S.dma("sp", [(lbr.ap[:, r_, :], w["lbl%d" % r_][:, 0:512].broadcast_to([16, 512])) for r_ in range(2)],
              [], [lbr], "cc")
        S.op("dve", lambda e: e.memset(zer.ap[:], 0.0), [], [zer])
        S.op("dve", lambda e: e.memset(d1z.ap[:], 0.0), [], [d1z])
        S.op("dve", lambda e: e.memset(nhalf.ap[:], -0.5), [], [nhalf])
        S.op("dve", lambda e: e.memset(Sf.ap[:], 0.0), [], [Sf])
        S.op("dve", lambda e: e.memset(Sb16.ap[:], 0.0), [], [Sb16])
        S.op("dve", lambda e: e.memset(Sb16m.ap[:], 0.0), [], [Sb16m])
        S.op("dve", lambda e: e.memset(kT.ap[:, :, 0:128], 0.0), [], [kT])
        S.op("dve", lambda e: e.memset(vtok[0].ap[:], 0.0), [], [vtok[0]])
        S.op("dve", lambda e: e.memset(ss.ap[:], 1.0), [], [ss])
    S.dma("sp", [(nks[:, 0:127, :], ck[:, 1:128, :])], [], ["nks_d"], "o_nks")
    S.dma("sp", [(nvs[:, 0:127, :], cv[:, 1:128, :])], [], ["nvs_d"], "o_nvs")

    PRE = [False]

    def tiles_of(seg):
        t = [(i, 128, i * 128) for i in range(NT)]
        if seg == 0:
            t.append((NT, NSAMP, TS))
        return t

    def blocks_of(seg):
        b = [(0, TS)]
        if seg == 0:
            b.append((TS, TS + NSAMP))
        return b

    def norm_to_hT(seg, gkey):
        S.phase = "%s%d:norm_%s" % ("P" if PRE[0] else "M", seg, gkey)
        tl = tiles_of(seg)
        g = gpre[gkey]

        def sq(i, tp, c0):
            act(junk.ap[0:tp, :], R[i].ap[0:tp, :], AF.Square, [R[i]], [junk, ("ssn", i)],
                accum_out=ss.ap[0:tp, 5 + i:6 + i])
            rsqrt(rstdn[0:tp, i:i + 1], ss.ap[0:tp, 5 + i:6 + i], 1.0 / D, EPS, [("ssn", i)], [("rstdn", i)],
                  rsq_tmpn[0:tp, i:i + 1], key=("rsqn", i))

        def fin(i, tp, c0):
            h_ = hb[i % 2]
            act(h_.ap[0:tp, :], R[i].ap[0:tp, :], AF.Copy, [R[i], ("rstdn", i)], [h_], scale=rstdn[0:tp, i:i + 1])
            pb = 6 + (i % 2)
            pv = psb(pb).rearrange("p (k n) -> p k n", k=8)

            def f(e, h_=h_, tp=tp, pv=pv):
                for k in range(8):
                    ins = e.transpose(out=pv[:, k, 0:tp], in_=h_.ap[0:tp, k * 128:(k + 1) * 128],
                                      identity=ident.ap[0:tp, 0:tp])
                return ins
            S.op("pe", f, [h_, ident], [PS[pb]])
            tt(hT_t[:, :, c0:c0 + tp], pv[:, :, 0:tp], g.ap[:].unsqueeze(2).broadcast_to([128, 8, tp]),
               ALU.mult, [PS[pb], g], [hT])
        for idx, t_ in enumerate(tl):
            sq(*t_)
            if idx >= 1:
                fin(*tl[idx - 1])
        fin(*tl[-1])

    def ffn_slots():
        sl = []
        for o in FSLOT:
            sl.append((arena(o, 4096, "p (k n) -> p k n", k=8),
                       arena(o + 4096, 4096, "p (k n) -> p k n", k=8),
                       arena(o + 8192, 4096, "p (k n) -> p k n", k=4)))
        return sl

    def ffn_load(wg, wu, wd, gi, si):
        h0, G = GROUPS[gi]
        a, b, c = ffn_slots()[si]
        S.dma("pool", [(a.ap[:, :, 0:G * 128], wg.rearrange("(k p) n -> p k n", p=128)[:, :, h0 * 128:(h0 + G) * 128]),
                       (b.ap[:, :, 0:G * 128], wu.rearrange("(k p) n -> p k n", p=128)[:, :, h0 * 128:(h0 + G) * 128]),
                       (c.ap[:, 0:G, :], wd.rearrange("(k p) n -> p k n", p=128)[:, h0:h0 + G, :])], [], [a, b, c],
              "wf%d" % si)

    def ffn(seg, wg, wu, wd, gpre_k, gpost_k, final, s0=0, preloaded=False, at_last=None, after_norm=None):
        tl = tiles_of(seg)
        bl = blocks_of(seg)
        norm_to_hT(seg, gpre_k)
        if after_norm is not None:
            after_norm()
        S.phase = "%s%d:ffn_%s" % ("P" if PRE[0] else "M", seg, gpre_k)
        wgv = wg.rearrange("(k p) n -> p k n", p=128)
        wuv = wu.rearrange("(k p) n -> p k n", p=128)
        wdv = wd.rearrange("(k p) n -> p k n", p=128)
        slots = ffn_slots()

        def load(gi):
            ffn_load(wg, wu, wd, gi, (gi + s0) % 2)
        if not preloaded:
            load(0)
        ui = 0
        for gi, (h0, G) in enumerate(GROUPS):
            if gi + 1 < len(GROUPS):
                load(gi + 1)
            elif at_last is not None:
                at_last()
            a, b, c = slots[(gi + s0) % 2]
            for (c0, c1) in bl:
                N = c1 - c0
                u = uT[ui % 2]
                ui += 1
                for j in range(G):
                    pg, pu = PS[(2 * j) % 4], PS[(2 * j + 1) % 4]
                    mm(pg.ap[:, 0:N], [(a.ap[:, k, j * 128:(j + 1) * 128], hT_t[:, k, c0:c1]) for k in range(8)],
                       [a, hT], [pg])
                    mm(pu.ap[:, 0:N], [(b.ap[:, k, j * 128:(j + 1) * 128], hT_t[:, k, c0:c1]) for k in range(8)],
                       [b, hT], [pu])
                    t_ = tmpA[j % 2]
                    act(t_.ap[:, 0:N], pg.ap[:, 0:N], AF.Silu, [pg], [t_])
                    tt(u.ap[:, j, 0:N], t_.ap[:, 0:N], pu.ap[:, 0:N], ALU.mult, [t_, pu], [akeys(u.off + j * 512, 512)])
                ukeys = [akeys(u.off, G * 512)]
                for (i, tp, tc0) in tl:
                    if not (c0 <= tc0 < c1):
                        continue
                    off = tc0 - c0
                    for half in range(2):
                        py = PS[4 + (2 * i + half) % 4]
                        mm(py.ap[0:tp, :], [(u.ap[:, j, off:off + tp], c.ap[:, j, half * 512:(half + 1) * 512])
                                            for j in range(G)], ukeys + [c], [py])
                        ysl = Y_t[0:tp, i, half * 512:(half + 1) * 512]
                        if gi == 0:
                            cp(ysl, py.ap[0:tp, :], [py], [ykeys(i, half)])
                        else:
                            tt(ysl, ysl, py.ap[0:tp, :], ALU.add, [py, ykeys(i, half)], [ykeys(i, half)])
        for (i, tp, c0) in tl:
            act(junk.ap[0:tp, :], Y_t[0:tp, i, :], AF.Square, [ykeys(i)], [junk, ("ss", i)],
                accum_out=ss.ap[0:tp, i:i + 1])
        n = len(tl)
        rsqrt(rstd.ap[:, 0:n], ss.ap[:, 0:n], 1.0 / D, EPS, [("ss", i) for i in range(n)] + [ss], [rstd],
              rsq_tmp[:, 0:n])
        gp = gpost[gpost_k]
        for (i, tp, c0) in tl:
            t_ = tmpD[i % 2]
            stt(t_.ap[0:tp, :], Y_t[0:tp, i, :], rstd.ap[0:tp, i:i + 1], gp.ap[0:tp, :], ALU.mult, ALU.mult,
                [ykeys(i), rstd, gp], [t_])
            tt(R[i].ap[0:tp, :], R[i].ap[0:tp, :], t_.ap[0:tp, :], ALU.add, [R[i], t_], [R[i]], eng="pool")
            if final:
                if tp == 128:
                    S.dma("sp", [(yp[seg * TS + c0: seg * TS + c0 + 128, :], R[i].ap[:, :])], [R[i]], [],
                          "o_y%d" % (i % 4))
                else:
                    S.dma("sp", [(ys[:, :], R[i].ap[0:tp, :])], [R[i]], [], "o_ys")

    def attention(seg):
        tl = tiles_of(seg)
        wq = wpre
        S.phase = "M%d:attn" % seg
        whg_load(with_f=(seg == 0))
        cp(sk.ap[:, :, 256:257], sinkb.ap[:].unsqueeze(2), [sinkb], [akeys(sk.off, 4224)])
        last_seg = (seg == NSEG - 1)
        for (c0, c1) in [(0, TS)]:
            for h in range(8):
                pq = PS[h % 2]
                mm(pq.ap[0:64, 0:512], [(wq.ap[:, k, h * 64:(h + 1) * 64], hT_t[:, k, c0:c1]) for k in range(8)],
                   [wq, hT], [pq])
                act(qT.ap[0:64, h, :], pq.ap[0:64, 0:512], AF.Copy, [pq], [akeys(qT.off + h * 512, 512)], scale=0.125)
            for j in range(2):
                pk = PS[2]
                mm(pk.ap[0:64, 0:512], [(wq.ap[:, k, 512 + j * 64:512 + (j + 1) * 64], hT_t[:, k, c0:c1])
                                        for k in range(8)], [wq, hT], [pk])
                cp(kT.ap[:, j, 128 + c0:128 + c1], pk.ap[0:64, 0:512], [pk], [kT])
            pv_ = PS[3]
            for (i, tp, tc0) in tl:
                if tp != 128:
                    continue
                mm(pv_.ap[:, i * 128:(i + 1) * 128], [(hT_t[:, k, tc0:tc0 + 128], wq.ap[:, k, 640:768]) for k in range(8)],
                   [wq, hT], [pv_])
                cp(vtok[i + 1].ap[:], pv_.ap[:, i * 128:(i + 1) * 128], [pv_], [vtok[i + 1]])
            if last_seg:
                i, tc0 = NT - 1, (NT - 1) * 128
                cp(kvf.ap[:, 128:256], pv_.ap[:, i * 128:(i + 1) * 128], [pv_], [("kvf", 1)])
                pk2 = PS[2]
                mm(pk2.ap[:, 0:128], [(hT_t[:, k, tc0:tc0 + 128], wq.ap[:, k, 512:640]) for k in range(8)],
                   [wq, hT], [pk2])
                cp(kvf.ap[:, 0:128], pk2.ap[:, 0:128], [pk2], [("kvf", 0)])
                S.dma("sp", [(nkp[:, :], kvf.ap[:, 0:128])], [("kvf", 0)], [], "o_nkp")
                S.dma("sp", [(nvp[:, :], kvf.ap[:, 128:256])], [("kvf", 1)], [], "o_nvp")

            def front(i, tc0):
                off = tc0 - c0
                pb_, st = pbfs[i % 2], st8s[i % 2]
                for h in range(8):
                    pb = PS[4 + h // 2]
                    mm(pb.ap[:, (h % 2) * 256:(h % 2) * 256 + 256],
                       [(qT.ap[0:64, h, off:off + 128], kT.ap[:, h // 4, tc0:tc0 + 256])],
                       [akeys(qT.off + h * 512, 512), kT], [pb])
                mk = mask0 if (seg == 0 and i == 0) else maskN
                for b4 in range(4):
                    pb = PS[4 + b4]
                    tt(sk.ap[:, 2 * b4:2 * b4 + 2, 0:256], pb.ap[:, :].rearrange("p (h k) -> p h k", h=2),
                       mk.ap[:].unsqueeze(1).broadcast_to([128, 2, 256]), ALU.add, [pb, mk], [skk(b4)])
                sk_k = [skk(b4) for b4 in range(4)]
                S.op("dve", lambda e: e.tensor_reduce(out=st["mx"].ap[:], in_=sk.ap[:, :, 0:257], axis=AX.X, op=ALU.max),
                     sk_k, [st["mx"]])
                ts(st["nmx"].ap[:], st["mx"].ap[:], -1.0, None, ALU.mult, None, [st["mx"]], [st["nmx"]])
                for h in range(8):
                    act(pb_.ap[:, h, 0:257], sk.ap[:, h, 0:257], AF.Exp, [skk(h // 2), st["nmx"]],
                        [akeys(pb_.off + h * 272, 272), st["sum"]], bias=st["nmx"].ap[:, h:h + 1],
                        accum_out=st["sum"].ap[:, h:h + 1])

            def back(i, tc0):
                pb_, st = pbfs[i % 2], st8s[i % 2]
                S.op("dve", lambda e: e.reciprocal(out=st["rinv"].ap[:], in_=st["sum"].ap[:]), [st["sum"]], [st["rinv"]])
                for half in range(2):
                    pvw = psb(half).rearrange("p (b n) -> p b n", b=8)

                    def f(e, half=half, pvw=pvw):
                        for q_ in range(8):
                            blk = half * 8 + q_
                            h, kb = blk // 2, blk % 2
                            ins = e.transpose(out=pvw[:, q_, :], in_=pb_.ap[:, h, kb * 128:(kb + 1) * 128],
                                              identity=ident.ap[:])
                        return ins
                    S.op("pe", f, [akeys(pb_.off + half * 1088, 1088), ident], [PS[half]])
                    if half == 0:
                        cp(pTs.ap[:, 0:8, :], pvw, [PS[0]], [akeys(pTs.off, 1024)])
                    else:
                        cp(pTs.ap[:, 8:16, :], pvw, [PS[1]], [akeys(pTs.off + 1024, 1024)])
                po = PS[2]

                def f(e, i=i):
                    for h in range(8):
                        a_ = h // 4
                        for kb in range(2):
                            ins = e.matmul(po.ap[:, h * 64:(h + 1) * 64], lhsT=pTs.ap[:, 2 * h + kb, :],
                                           rhs=vtok[i + kb].ap[:, a_ * 64:(a_ + 1) * 64], start=(kb == 0),
                                           stop=(kb == 1))
                    return ins
                S.op("pe", f, [pTs, vtok[i], vtok[i + 1]], [po])
                tt(obf.ap[:].rearrange("p (h d) -> p h d", h=8), po.ap[:, :].rearrange("p (h d) -> p h d", h=8),
                   st["rinv"].ap[:].unsqueeze(2).broadcast_to([128, 8, 64]), ALU.mult, [po, st["rinv"]], [obf])
                pvw = psb(3).rearrange("p (b n) -> p b n", b=8)

                def f(e, pvw=pvw):
                    for c_ in range(4):
                        ins = e.transpose(out=pvw[:, c_, :], in_=obf.ap[:, c_ * 128:(c_ + 1) * 128],
                                          identity=ident.ap[:])
                    return ins
                S.op("pe", f, [obf, ident], [PS[3]])
                cp(oaT_t[:, :, tc0:tc0 + 128], pvw[:, 0:4, :], [PS[3]], [oaT])

            ptl = [(i, tc0) for (i, tp, tc0) in tl if tp == 128]
            front(*ptl[0])
            for n_, (i, tc0) in enumerate(ptl):
                if n_ + 1 < len(ptl):
                    front(*ptl[n_ + 1])
                back(i, tc0)
        cp(kT.ap[:, :, 0:128], kT.ap[:, :, TS:TS + 128], [kT], [kT])
        cp(vtok[0].ap[:], vtok[NT].ap[:], [vtok[NT]], [vtok[0]])
        if seg == 0:
            sample_attention(wq)

    def sample_attention(wq):
        sc0 = TS
        S.phase = "M0:sattn"
        pq = PS[0]

        def f(e):
            for h in range(8):
                for k in range(8):
                    ins = e.matmul(pq.ap[0:64, h * 16:(h + 1) * 16], lhsT=wq.ap[:, k, h * 64:(h + 1) * 64],
                                   rhs=hT_t[:, k, sc0:sc0 + NSAMP], start=(k == 0), stop=(k == 7))
            return ins
        S.op("pe", f, [wq, hT], [pq])
        act(qs.ap[:].rearrange("p h b -> p (h b)"), pq.ap[0:64, 0:128], AF.Copy, [pq], [qs], scale=0.125)
        pk = PS[1]
        mm(pk.ap[0:NSAMP, 0:256], [(hT_t[:, k, sc0:sc0 + NSAMP], wq.ap[:, k, 512:768]) for k in range(8)],
           [wq, hT], [pk])
        cp(kvf.ap[0:NSAMP, :], pk.ap[0:NSAMP, 0:256], [pk], [("kvf", 0), ("kvf", 1)])
        S.dma("sp", [(nks[:, 127, :], kvf.ap[0:NSAMP, 0:128])], [("kvf", 0)], ["nks_d"], "o_nks2")
        S.dma("sp", [(nvs[:, 127, :], kvf.ap[0:NSAMP, 128:256])], [("kvf", 1)], ["nvs_d"], "o_nvs2")
        S.dma("pool", [(Ks.ap[:], nks.rearrange("b w n -> w b n"))], ["nks_d"], [Ks], "ld_ks")
        S.dma("pool", [(Vsj[j].ap[:], nvs.rearrange("b w (j d) -> w b j d", j=2)[:, :, j, :]) for j in range(2)],
              ["nvs_d"], Vsj, "ld_vs")
        for j in range(2):
            for half in range(2):
                pvw = psb(2 + half).rearrange("p (b n) -> p b n", b=8)

                def f(e, half=half, pvw=pvw, j=j):
                    for q_ in range(8):
                        b_ = half * 8 + q_
                        ins = e.transpose(out=pvw[0:64, q_, :], in_=Ks.ap[:, b_, j * 64:(j + 1) * 64],
                                          identity=ident.ap[:])
                    return ins
                S.op("pe", f, [Ks, ident], [PS[2 + half]])
                cp(KTs.ap[0:64, half * 1024:(half + 1) * 1024], psb(2 + half)[0:64, :], [PS[2 + half]], [KTs])
            for n_ in range(4):
                pb = PS[4 + n_]
                mm(pb.ap[0:64, :], [(qs.ap[:].rearrange("p h b -> p (h b)")[:, 64 * j:64 * j + 64], KTs.ap[0:64, n_ * 512:(n_ + 1) * 512])], [qs, KTs], [pb])
                tt(smk.ap[0:64, 2 * n_:2 * n_ + 2, :].rearrange("p a (b k) -> p (a b) k", b=2),
                   pb.ap[0:64, :].rearrange("p (b k) -> p b k", b=4),
                   dm.ap[:, 4 * n_:4 * n_ + 4].unsqueeze(2).broadcast_to([64, 4, 128]), ALU.mult, [pb, dm],
                   [smkk(smk, n_)])
            smk_k = [smkk(smk, b4) for b4 in range(4)]
            S.op("dve", lambda e: e.tensor_reduce(
                out=s_s.ap[:], in_=smk.ap[0:64, :, :].rearrange("p a (b k) -> p k (a b)", b=2), axis=AX.X,
                op=ALU.add), smk_k, [s_s])
            m = st1
            S.op("dve", lambda e: e.tensor_reduce(out=m["mx"].ap[:], in_=s_s.ap[:], axis=AX.X, op=ALU.max), [s_s],
                 [m["mx"]])
            tt(m["mx"].ap[:], m["mx"].ap[:], sinkc.ap[:, j:j + 1], ALU.max, [m["mx"], sinkc], [m["mx"]])
            ts(m["nmx"].ap[:], m["mx"].ap[:], -1.0, None, ALU.mult, None, [m["mx"]], [m["nmx"]])
            act(p_s.ap[:], s_s.ap[:], AF.Exp, [s_s, m["nmx"]], [p_s, m["sum"]], bias=m["nmx"].ap[:, 0:1],
                accum_out=m["sum"].ap[:, 0:1])
            tt(m["es"].ap[:], sinkc.ap[:, j:j + 1], m["mx"].ap[:], ALU.subtract, [sinkc, m["mx"]], [m["es"]])
            act(m["es"].ap[:], m["es"].ap[:], AF.Exp, [m["es"]], [m["es"]])
            tt(m["es"].ap[:], m["es"].ap[:], m["sum"].ap[:], ALU.add, [m["es"], m["sum"]], [m["es"]])
            S.op("dve", lambda e: e.reciprocal(out=m["rinv"].ap[:], in_=m["es"].ap[:]), [m["es"]], [m["rinv"]])
            pvw = psb(0)
            S.op("pe", lambda e: e.transpose(out=pvw[:, 0:64], in_=p_s.ap[:], identity=ident.ap[0:64, 0:64]),
                 [p_s, ident], [PS[0]])
            cp(pT_s.ap[:], pvw[:, 0:64], [PS[0]], [pT_s])
            for n_ in range(2):
                pb = PS[2 + n_]
                mm(pb.ap[0:64, :], [(pT_s.ap[:], Vsj[j].ap[:, 8 * n_:8 * n_ + 8, :].rearrange("p b d -> p (b d)"))], [pT_s, Vsj[j]], [pb])
                tt(smk.ap[0:64, 2 * n_:2 * n_ + 2, :].rearrange("p a (b k) -> p (a b) k", b=4),
                   pb.ap[0:64, :].rearrange("p (b k) -> p b k", b=8),
                   dm.ap[:, 8 * n_:8 * n_ + 8].unsqueeze(2).broadcast_to([64, 8, 64]), ALU.mult, [pb, dm],
                   [smkk(smk, n_)])
            S.op("dve", lambda e: e.tensor_reduce(
                out=o_s.ap[:], in_=smk.ap[0:64, 0:4, :].rearrange("p a (b k) -> p k (a b)", b=4), axis=AX.X,
                op=ALU.add), [smkk(smk, 0), smkk(smk, 1)], [o_s])
            ts(o_sb.ap[:], o_s.ap[:], m["rinv"].ap[:, 0:1], None, ALU.mult, None, [o_s, m["rinv"]], [o_sb])
            S.op("pe", lambda e: e.transpose(out=pvw[0:64, 128:192], in_=o_sb.ap[:], identity=ident.ap[0:64, 0:64]),
                 [o_sb, ident], [PS[0]])
            cp(oTs.ap[:, 4 * j:4 * j + 4, :].rearrange("p g b -> p (g b)"), pvw[0:64, 128:192], [PS[0]], [oTs])

    def kv_tail():
        wkv = arena(0, 8 * 256, "p (k n) -> p k n", k=8)
        winv = w["win"].rearrange("(k p) n -> p k n", p=128)
        S.dma("pool", [(wkv.ap[:], winv[:, :, 512:768])], [], [wkv], "wkv")
        tc0 = TS - 128
        for j in range(2):
            pk = PS[2]
            mm(pk.ap[0:64, 0:128], [(wkv.ap[:, k, j * 64:(j + 1) * 64], hT_t[:, k, tc0:tc0 + 128]) for k in range(8)],
               [wkv, hT], [pk])
            cp(tmpD[1].ap[0:64, 128 + j * 128:256 + j * 128], pk.ap[0:64, 0:128], [pk], [tmpD[1]])
        pv_ = PS[3]
        mm(pv_.ap[:, 0:128], [(hT_t[:, k, tc0:tc0 + 128], wkv.ap[:, k, 128:256]) for k in range(8)], [wkv, hT], [pv_])
        cp(tmpD[1].ap[:, 0:128], pv_.ap[:, 0:128], [pv_], [tmpD[1]])

    def whg_buf():
        return arena(0, 8 * 2048, "p (k n) -> p k n", k=8)

    def whg_load(with_f=True):
        wv = w["win"].rearrange("(k p) n -> p k n", p=128)
        if with_f:
            S.dma("pool", [(whg_buf().ap[:], wv[:, :, 768:2816])], [], [whg_buf()], "whg")
        else:
            S.dma("pool", [(whg_buf().ap[:, :, 0:512], wv[:, :, 768:1280]),
                           (whg_buf().ap[:, :, 1024:2048], wv[:, :, 1792:2816])], [], [whg_buf()], "whg")

    def wov_buf():
        return arena(26624, 8192, "p (k n) -> p k n", k=8)

    def wov_load():
        S.dma("pool", [(wov_buf().ap[:], w["wout"].rearrange("(k p) n -> p k n", p=128))], [], [wov_buf()], "wov")

    def hgrn(seg, state_only=False):
        tl = tiles_of(seg)
        S.phase = "%s%d:hgrn" % ("P" if PRE[0] else "M", seg)
        winv = w["win"].rearrange("(k p) n -> p k n", p=128)
        if state_only:
            whg, fo = wpre, 0
        else:
            whg, fo = whg_buf(), 512
            if seg >= 1:
                wov_load()
        io = fo + 512
        pend = [None]
        pm_blk = AR_t[:, THG:THG + 4096].bitcast(F32)
        kk_blk = AR_t[:, THG + 6144:THG + 10240]
        k2_blk = kd2c0.ap[:].rearrange("p h n -> p (h n)")
        pt_blk = Pt.ap[:].rearrange("p h t -> p (h t)")
        kkeys = Pm + keT + kdT + [("kd2c0", h_) for h_ in range(4)] + [("Pt", h_) for h_ in range(4)]
        if not state_only:
            S.dma("sp", [(pm_blk, spP[seg]), (kk_blk, spK[seg]), (k2_blk, spK2[seg]), (pt_blk, spPt[seg])],
                  [("spk", seg)], kkeys, "ld_spk")
        last_seg = (seg == NSEG - 1) and not state_only
        c0, c1 = 0, TS
        for h in range(4):
            pf, pq = PS[2 * (h % 2)], PS[2 * (h % 2) + 1]
            if state_only:
                mm(pf.ap[:, :], [(whg.ap[:, k, fo + h * 128:fo + (h + 1) * 128], hT_t[:, k, c0:c1]) for k in range(8)],
                   [whg, hT], [pf])
            if not state_only:
                mm(pq.ap[:, :], [(whg.ap[:, k, h * 128:(h + 1) * 128], hT_t[:, k, c0:c1]) for k in range(8)],
                   [whg, hT], [pq])
            tf, kh_, fv = tmpA[h % 2], tmpB[h % 2], tmpC[h % 2]
            if state_only:
                act(tf.ap[:], pf.ap[:, :], AF.Tanh, [pf], [tf], scale=0.5)
                ts(kh_.ap[:], tf.ap[:], ccn.ap[:, h:h + 1], cc.ap[:, h:h + 1], ALU.mult, ALU.add, [tf, ccn, cc], [kh_])
                act(fv.ap[:], tf.ap[:], AF.Identity, [tf, cc, cc1], [fv], scale=cc.ap[:, h:h + 1], bias=cc1.ap[:, h:h + 1])
                P = Pm[h]

                if "s" in FEAT:
                    fvv = fv.ap.rearrange("p (c j) -> p c j", j=64)
                    d1v = d1z.ap.rearrange("p (c j) -> p c j", j=64)
                    S.op("dve", lambda e, fvv=fvv, d1v=d1v: e.tensor_copy(out=d1v[:, :, 0:1], in_=fvv[:, :, 0:1]), [fv], [d1z])
                    S.op("dve", lambda e, fvv=fvv: e.memset(fvv[:, :, 0:1], 0.0), [], [fv])
                    S.op("dve", lambda e, fv=fv, P=P: e.tensor_tensor_scan(out=P.ap[:, :], data0=fv.ap[:, :], data1=d1z.ap[:, :],
                                                                            initial=1.0, op0=ALU.mult, op1=ALU.add),
                         [fv, d1z], [P])
                else:
                    def f(e, fv=fv, P=P):
                        for c_ in range(8):
                            ins = e.tensor_tensor_scan(out=P.ap[:, c_ * 64:(c_ + 1) * 64], data0=fv.ap[:, c_ * 64:(c_ + 1) * 64],
                                                       data1=zer.ap[:, :], initial=1.0, op0=ALU.mult, op1=ALU.add)
                        return ins
                    S.op("dve", f, [fv, zer], [P])
                S.op("dve", lambda e, fv=fv, P=P: e.reciprocal(out=fv.ap[:], in_=P.ap[:]), [P], [fv])
                tt(keT[h].ap[:], kh_.ap[:], fv.ap[:], ALU.mult, [kh_, fv], [keT[h]])
                Pv = P.ap.rearrange("p (t c j) -> p t c j", c=2, j=64)
                kev = keT[h].ap.rearrange("p (t c j) -> p t c j", c=2, j=64)
                kdv = kdT[h].ap.rearrange("p (t c j) -> p t c j", c=2, j=64)
                tt(Pt.ap[:, h, :], Pv[:, :, 0, 63], Pv[:, :, 1, 63], ALU.mult, [P], [("Pt", h)])

                def f(e, h=h, kev=kev, kdv=kdv, Pv=Pv):
                    ins = None
                    if "k" in FEAT:
                        ins = e.tensor_tensor(out=kdv, in0=kev, in1=Pv[:, :, :, 63:64].broadcast_to([128, NT, 2, 64]), op=ALU.mult)
                    if "2" in FEAT:
                        ins = e.tensor_tensor(out=kd2c0.ap[:, h, :].rearrange("p (t j) -> p t j", j=64), in0=kev[:, :, 0, :],
                                              in1=Pt.ap[:, h, :].unsqueeze(2).broadcast_to([128, NT, 64]), op=ALU.mult)
                    for t_ in range(NT):
                        if "k" not in FEAT:
                            if not state_only:
                                e.tensor_scalar(out=kdv[:, t_, 0, :], in0=kev[:, t_, 0, :], scalar1=Pv[:, t_, 0, 63:64],
                                                scalar2=None, op0=ALU.mult)
                            ins = e.tensor_scalar(out=kdv[:, t_, 1, :], in0=kev[:, t_, 1, :], scalar1=Pv[:, t_, 1, 63:64],
                                                  scalar2=None, op0=ALU.mult)
                        if "2" not in FEAT:
                            ins = e.tensor_scalar(out=kd2c0.ap[:, h, t_ * 64:(t_ + 1) * 64], in0=kev[:, t_, 0, :],
                                                  scalar1=Pt.ap[:, h, t_:t_ + 1], scalar2=None, op0=ALU.mult)
                    return ins
                S.op("dve", f, [keT[h], P, ("Pt", h)], [kdT[h], ("kd2c0", h)])
            P = Pm[h]
            Pv = P.ap.rearrange("p (t c j) -> p t c j", c=2, j=64)
            if state_only:
                continue
            act(tf.ap[:], pq.ap[:, :], AF.Tanh, [pq], [tf], scale=0.5)
            stt(kh_.ap[:], tf.ap[:], 1.0, pq.ap[:, :], ALU.add, ALU.mult, [tf, pq], [kh_])
            stt(qeT[h].ap[:], kh_.ap[:], 0.5, P.ap[:], ALU.mult, ALU.mult, [kh_, P], [qeT[h]])
            qev = qeT[h].ap.rearrange("p (t c j) -> p t c j", c=2, j=64)

            def f(e, h=h, qev=qev, Pv=Pv):
                if "q" in FEAT:
                    return e.tensor_tensor(out=qe2c1.ap[:, h, :].rearrange("p (t j) -> p t j", j=64), in0=qev[:, :, 1, :],
                                           in1=Pv[:, :, 0, 63:64].broadcast_to([128, NT, 64]), op=ALU.mult)
                for t_ in range(NT):
                    ins = e.tensor_scalar(out=qe2c1.ap[:, h, t_ * 64:(t_ + 1) * 64], in0=qev[:, t_, 1, :],
                                          scalar1=Pv[:, t_, 0, 63:64], scalar2=None, op0=ALU.mult)
                return ins
            S.op("dve", f, [qeT[h], P], [("qe2c1", h)])
        if state_only:
            S.dma("sp", [(spP[seg], pm_blk), (spK[seg], kk_blk), (spK2[seg], k2_blk), (spPt[seg], pt_blk)],
                  kkeys, [("spk", seg)], "st_spk")
        for (i, tp, tc0) in tl:
            if tp != 128:
                continue
            off = tc0 - c0
            pvw = psb(2).rearrange("p (b n) -> p b n", b=8)

            def f(e, off=off, pvw=pvw, i=i):
                for h in range(4):
                    e.transpose(out=pvw[0:64, h, :], in_=kd2c0.ap[:, h, i * 64:(i + 1) * 64], identity=ident.ap[:])
                    ins = e.transpose(out=pvw[64:128, h, :], in_=kdT[h].ap[:, off + 64:off + 128], identity=ident.ap[:])
                return ins
            S.op("pe", f, kdT + [("kd2c0", h) for h in range(4)] + [ident], [PS[2]])
            S.op("act", lambda e, pvw=pvw: e.copy(out=kdtok.ap[:], in_=pvw[:, 0:4, :]), [PS[2]], [kdtok])
            pi_, pg_, pu_ = PS[5], PS[6], PS[0]
            mm(pi_.ap[:, :], [(hT_t[:, k, tc0:tc0 + 128], whg.ap[:, k, io:io + 512]) for k in range(8)], [whg, hT], [pi_])
            S.op("act", lambda e: e.copy(out=vh.ap[:], in_=pi_.ap[:, :]), [pi_], [vh])

            def f(e):
                for h in range(4):
                    ins = e.matmul(pu_.ap[:, h * 128:(h + 1) * 128], lhsT=kdtok.ap[:, h, :], rhs=vh.ap[:, h * 128:(h + 1) * 128],
                                   start=True, stop=True)
                return ins
            S.op("pe", f, [kdtok, vh], [pu_])
            if not state_only:
                sb_old, sb_new = SB[par[0]], SB[1 - par[0]]
                mm(pg_.ap[:, :], [(hT_t[:, k, tc0:tc0 + 128], whg.ap[:, k, 1536:2048]) for k in range(8)], [whg, hT], [pg_])
                tg = tmpD[0]
                act(tg.ap[:, 0:512], pg_.ap[:, :], AF.Tanh, [pg_], [tg], scale=0.5)
                stt(sgg.ap[:], tg.ap[:, 0:512], 1.0, pg_.ap[:, :], ALU.add, ALU.mult, [tg, pg_], [sgg])
                pa, pa2 = PS[7], PS[1]

                def f(e, off=off):
                    for h in range(4):
                        e.matmul(pa.ap[:, h * 128:(h + 1) * 128], lhsT=keT[h].ap[:, off:off + 128],
                                 rhs=qeT[h].ap[:, off:off + 128], start=True, stop=True)
                    for h in range(4):
                        ins = e.matmul(pa2.ap[0:64, h * 64:(h + 1) * 64], lhsT=kdT[h].ap[:, off:off + 64],
                                       rhs=qeT[h].ap[:, off + 64:off + 128], start=True, stop=True)
                    return ins
                S.op("pe", f, keT + qeT + kdT, [pa, pa2])
                if pend[0] is not None:
                    pend[0]()
                    pend[0] = None
                tt(ATs.ap[:], pa.ap[:, :].rearrange("p (h t) -> p h t", h=4),
                   tri.ap[:].unsqueeze(1).broadcast_to([128, 4, 128]), ALU.mult, [pa, tri], [ATs])
                cp(ATs.ap[0:64, :, 64:128], pa2.ap[0:64, 0:256].rearrange("p (h t) -> p h t", h=4), [pa2, ATs], [ATs])
                po = PS[3 + (i % 2)]

                def f(e, off=off, i=i, po=po, sb_old=sb_old):
                    for h in range(4):
                        hc = slice(h * 128, (h + 1) * 128)
                        e.matmul(po.ap[:, hc], lhsT=ATs.ap[:, h, :], rhs=vh.ap[:, hc], start=True, stop=False)
                        e.matmul(po.ap[0:64, hc], lhsT=qeT[h].ap[:, off:off + 64], rhs=sb_old.ap[:, h, :],
                                 start=False, stop=True, skip_group_check=True)
                        ins = e.matmul(po.ap[64:128, hc], lhsT=qe2c1.ap[:, h, i * 64:(i + 1) * 64], rhs=sb_old.ap[:, h, :],
                                       start=False, stop=True, skip_group_check=True)
                    return ins
                S.op("pe", f, [ATs, vh, sb_old] + qeT + [("qe2c1", h) for h in range(4)], [po])
            for h in range(4):
                stt(Sf.ap[:, h, :], Sf.ap[:, h, :], Pt.ap[:, h, i:i + 1], pu_.ap[:, h * 128:(h + 1) * 128], ALU.mult, ALU.add,
                    [("Sf", h), ("Pt", h), pu_], [("Sf", h)])
            if not state_only:
                S.op("act", lambda e, sb_new=sb_new: e.copy(out=sb_new.ap[:], in_=Sf.ap[:]),
                     [("Sf", h) for h in range(4)], [sb_new])
                par[0] ^= 1
                pend[0] = hgrn_epilogue(po, [po], 128, tc0, defer=True)
        if pend[0] is not None:
            pend[0]()
            pend[0] = None
        if last_seg:
            S.dma("sp", [(nhp.rearrange("h k v -> k h v"), Sf.ap[:])], [("Sf", h) for h in range(4)], [], "o_nhp")
        if seg == 0 and not state_only:
            sample_hgrn(whg)

    def hgrn_epilogue(po, pokeys, tp, tc0, defer=False):
        for h in range(4):
            act(junk.ap[0:tp, 0:128], po.ap[0:tp, h * 128:(h + 1) * 128], AF.Square, pokeys, [junk, ("ssh", h)],
                accum_out=ssh.ap[0:tp, h:h + 1])
        rsqrt(rsh.ap[0:tp, :], ssh.ap[0:tp, :], 1.0 / 128, EPS, [("ssh", h) for h in range(4)], [rsh],
              rsq_tmp[0:tp, 0:4])
        t_ = tmpD[1]
        for h in range(4):
            stt(t_.ap[0:tp, h * 128:(h + 1) * 128], po.ap[0:tp, h * 128:(h + 1) * 128], rsh.ap[0:tp, h:h + 1],
                hgnb.ap[0:tp, :], ALU.mult, ALU.mult, pokeys + [rsh, hgnb], [t_])
        tt(ohn.ap[0:tp, :], t_.ap[0:tp, 0:512], sgg.ap[0:tp, :], ALU.mult, [sgg, t_], [ohn])
        pvw = psb(2).rearrange("p (b n) -> p b n", b=8)

        def part2():
            def f(e):
                for c_ in range(4):
                    ins = e.transpose(out=pvw[:, c_, 0:tp], in_=ohn.ap[0:tp, c_ * 128:(c_ + 1) * 128],
                                      identity=ident.ap[0:tp, 0:tp])
                return ins
            S.op("pe", f, [ohn, ident], [PS[2]])
            cp(ohT_t[:, :, tc0:tc0 + tp], pvw[:, 0:4, 0:tp], [PS[2]], [ohT])
        if defer:
            return part2
        part2()

    def sample_hgrn(whg):
        sc0 = TS
        S.phase = "M0:shgrn"
        B = NSAMP
        for h in range(4):
            pf = PS[h % 2]
            mm(pf.ap[:, 0:B], [(whg.ap[:, k, 512 + h * 128:512 + (h + 1) * 128], hT_t[:, k, sc0:sc0 + B]) for k in range(8)],
               [whg, hT], [pf])
            mm(pf.ap[:, 64:64 + B], [(whg.ap[:, k, h * 128:(h + 1) * 128], hT_t[:, k, sc0:sc0 + B]) for k in range(8)],
               [whg, hT], [pf])
            tf = tmpA[h % 2]
            act(tf.ap[:, 0:B], pf.ap[:, 0:B], AF.Tanh, [pf], [tf], scale=0.5)
            act(fvs.ap[:, h, :], tf.ap[:, 0:B], AF.Identity, [tf, cc, cc1], [fvs], scale=cc.ap[:, h:h + 1],
                bias=cc1.ap[:, h:h + 1])
            act(tf.ap[:, 64:64 + B], pf.ap[:, 64:64 + B], AF.Tanh, [pf], [tf], scale=0.5)
            stt(qss.ap[:, h, :], tf.ap[:, 64:64 + B], 1.0, pf.ap[:, 64:64 + B], ALU.add, ALU.mult, [tf, pf], [qss])
        ts(qss.ap[:], qss.ap[:], 0.5, None, ALU.mult, None, [qss], [qss])
        pf, pi_, pg_ = PS[2], PS[3], PS[4]
        mm(pf.ap[0:B, :], [(hT_t[:, k, sc0:sc0 + B], whg.ap[:, k, 512:1024]) for k in range(8)], [whg, hT], [pf])
        mm(pi_.ap[0:B, :], [(hT_t[:, k, sc0:sc0 + B], whg.ap[:, k, 1024:1536]) for k in range(8)], [whg, hT], [pi_])
        mm(pg_.ap[0:B, :], [(hT_t[:, k, sc0:sc0 + B], whg.ap[:, k, 1536:2048]) for k in range(8)], [whg, hT], [pg_])
        tf = tmpA[0]
        act(tf.ap[0:B, :], pf.ap[0:B, :], AF.Tanh, [pf], [tf], scale=0.5)
        tB = tmpB[0]
        tt(tB.ap[0:B, :], tf.ap[0:B, :], ccrn.ap[:], ALU.mult, [tf, ccrn], [tB])
        tt(khtok.ap[:], tB.ap[0:B, :], ccr.ap[:], ALU.add, [tB, ccr], [khtok])
        tg = tmpD[0]
        act(tg.ap[0:B, 0:512], pg_.ap[0:B, :], AF.Tanh, [pg_], [tg], scale=0.5)
        stt(sgg.ap[0:B, :], tg.ap[0:B, 0:512], 1.0, pg_.ap[0:B, :], ALU.add, ALU.mult, [tg, pg_], [sgg])
        for n_ in range(4):
            Sg = (Ssg, Ssg2)[n_ % 2]
            S.dma("sp", [(Sg.ap[:, h, :, :], st[4 * n_:4 * n_ + 4, h].rearrange("b k v -> k b v")) for h in range(4)], [], [Sg], "ld_st%d" % (n_ % 2))
            Ss4 = Sg.ap
            for h in range(4):
                ve = vexp[h]
                if n_ == 0:
                    tt(ve.ap[0:B, :, :], pi_.ap[0:B, h * 128:(h + 1) * 128].unsqueeze(1).broadcast_to([B, 16, 128]),
                       id16.ap[:].unsqueeze(2).broadcast_to([B, 16, 128]), ALU.mult, [pi_, id16], [ve])
                pu_ = PS[5 + (h % 2)]
                mm(pu_.ap[:, :], [(khtok.ap[0:B, h * 128:(h + 1) * 128], ve.ap[0:B, 4 * n_:4 * n_ + 4, :].rearrange("p b v -> p (b v)"))],
                   [khtok, ve], [pu_])
                for bb in range(4):
                    b_ = 4 * n_ + bb
                    stt(Ss4[:, h, bb, :], Ss4[:, h, bb, :], fvs.ap[:, h, b_:b_ + 1],
                        pu_.ap[:, bb * 128:(bb + 1) * 128], ALU.mult, ALU.add, [Sg, fvs, pu_], [Sg])
            S.dma("sp", [(nhs[4 * n_:4 * n_ + 4, h].rearrange("b k v -> k b v"), Sg.ap[:, h, :, :]) for h in range(4)], [Sg], [], "o_nhs%d" % (n_ % 2))
            for h in range(4):
                pb = PS[h % 2]
                mm(pb.ap[0:B, :], [(qss.ap[:, h, :], Ss4[:, h, :, :].rearrange("p b v -> p (b v)"))], [qss, Sg], [pb])
                tt(smk2.ap[0:B, 0:2, :].rearrange("p a (b k) -> p (a b) k", b=2),
                   pb.ap[0:B, :].rearrange("p (b k) -> p b k", b=4),
                   id16.ap[:, 4 * n_:4 * n_ + 4].unsqueeze(2).broadcast_to([B, 4, 128]), ALU.mult, [pb, id16],
                   [smkk(smk2, 0)])
                S.op("dve", lambda e, h=h, n_=n_: e.tensor_reduce(
                    out=smk2.ap[0:B, 4 + (n_ % 2) * 2 + h // 2, (h % 2) * 128:(h % 2) * 128 + 128],
                    in_=smk2.ap[0:B, 0:2, :].rearrange("p a (b k) -> p k (a b)", b=2), axis=AX.X, op=ALU.add),
                    [smkk(smk2, 0)], [smkk(smk2, 2 + (n_ % 2))])
            src = smk2.ap[0:B, 4 + (n_ % 2) * 2:6 + (n_ % 2) * 2, :].rearrange("p a k -> p (a k)")
            if n_ == 0:
                cp(o_hs.ap[:], src, [smkk(smk2, 2 + (n_ % 2))], [o_hs])
            else:
                tt(o_hs.ap[:], o_hs.ap[:], src, ALU.add, [smkk(smk2, 2 + (n_ % 2)), o_hs], [o_hs])
        hgrn_epilogue(o_hs, [o_hs], B, TS)

    def mixout(seg):
        tl = tiles_of(seg)
        S.phase = "M%d:mixout" % seg
        bl = blocks_of(seg)
        winv = w["win"].rearrange("(k p) n -> p k n", p=128)
        wov = wov_buf()
        mT = mT0 if seg == 0 else mT1
        MC = NC if seg == 0 else TS
        wao64 = arena(10752, 8192, "p (h n) -> p h n", h=8)
        if seg == 0:
            wov_load()
            S.dma("pool", [(wao64.ap[0:64, :, :], w["wao"].rearrange("(h p) n -> p h n", p=64))], [], [wao64], "wao64")
        slots = [(arena(0, 4096, "p (k a n) -> p k a n", k=8, a=2), arena(4096, 1024, "p (c n) -> p c n", c=4),
                  arena(5120, 1024, "p (c n) -> p c n", c=4)),
                 (arena(22528, 4096, "p (k a n) -> p k a n", k=8, a=2), arena(34816, 1024, "p (c n) -> p c n", c=4),
                  arena(35840, 1024, "p (c n) -> p c n", c=4))]

        def load(m_):
            a, b, c = slots[m_ % 2]
            cs = slice(m_ * 256, (m_ + 1) * 256)
            S.dma("pool", [(a.ap[:, :, 0, :], winv[:, :, 2816 + m_ * 256:2816 + (m_ + 1) * 256]),
                           (a.ap[:, :, 1, :], winv[:, :, 3840 + m_ * 256:3840 + (m_ + 1) * 256]),
                           (b.ap[:], w["wao"].rearrange("(c p) n -> p c n", p=128)[:, :, cs]),
                           (c.ap[:], w["who"].rearrange("(c p) n -> p c n", p=128)[:, :, cs])],
                  [], [a, b, c], "wm%d" % (m_ % 2))
        load(0)
        for n_ in range(8):
            m_, nn = n_ // 2, n_ % 2
            ns = slice(nn * 128, (nn + 1) * 128)
            if nn == 0 and m_ + 1 < 4:
                load(m_ + 1)
            if n_ == 5 and seg >= 1 and stage >= 3:
                ffn_load(w["w2g"], w["w2u"], w["w2d"], 0, 0)
            a, b, c = slots[m_ % 2]
            for (c0, c1) in bl:
                N = c1 - c0
                pb0 = 4 * (n_ % 2)
                pga, pgh, ppa, pph = PS[pb0], PS[pb0 + 1], PS[pb0 + 2], PS[pb0 + 3]
                mm(pga.ap[:, 0:N], [(a.ap[:, k, 0, ns], hT_t[:, k, c0:c1]) for k in range(8)], [a, hT], [pga])
                mm(pgh.ap[:, 0:N], [(a.ap[:, k, 1, ns], hT_t[:, k, c0:c1]) for k in range(8)], [a, hT], [pgh])
                if c0 < TS:
                    mm(ppa.ap[:, 0:N], [(b.ap[:, c_, ns], oaT_t[:, c_, c0:c1]) for c_ in range(4)], [b, oaT], [ppa])
                else:
                    mm(ppa.ap[:, 0:N], [(wao64.ap[0:64, h, n_ * 128:(n_ + 1) * 128], oTs.ap[:, h, :]) for h in range(8)],
                       [wao64, oTs], [ppa])
                mm(pph.ap[:, 0:N], [(c.ap[:, c_, ns], ohT_t[:, c_, c0:c1]) for c_ in range(4)], [c, ohT], [pph])
                ta, th = tmpA[n_ % 2], tmpB[n_ % 2]
                act(ta.ap[:, 0:N], pga.ap[:, 0:N], AF.Tanh, [pga], [ta], scale=0.5)
                act(th.ap[:, 0:N], pgh.ap[:, 0:N], AF.Tanh, [pgh], [th], scale=0.5)
                m1, m2 = (tmpC[0], tmpC[1]) if n_ % 2 == 0 else (tmpD[0], tmpD[1])
                stt(m1.ap[:, 0:N], ta.ap[:, 0:N], 1.0, ppa.ap[:, 0:N], ALU.add, ALU.mult, [ta, ppa], [m1])
                stt(m2.ap[:, 0:N], th.ap[:, 0:N], 1.0, pph.ap[:, 0:N], ALU.add, ALU.mult, [th, pph], [m2])
                tt(mT.ap[:, n_, c0:c1], m1.ap[:, 0:N], m2.ap[:, 0:N], ALU.add, [m1, m2], [akeys(mT.off + n_ * MC, MC)], eng="pool")
        mkeys = [mT]
        gp = gpost["nmpost"]
        for (i, tp, tc0) in tl:
            for half in range(2):
                py = PS[4 + (2 * i + half) % 4]
                mm(py.ap[0:tp, :], [(mT.ap[:, n_, tc0:tc0 + tp], wov.ap[:, n_, half * 512:(half + 1) * 512])
                                    for n_ in range(8)], mkeys + [wov], [py])
                act(junk.ap[0:tp, 0:512], py.ap[0:tp, :], AF.Square, [py], [junk, ("ss2", half)],
                    accum_out=ss.ap[0:tp, 12 + half:13 + half])
            tt(ss.ap[0:tp, 14:15], ss.ap[0:tp, 12:13], ss.ap[0:tp, 13:14], ALU.add, [("ss2", 0), ("ss2", 1)], [("ss2", 2)])
            rsqrt(rstd.ap[0:tp, 15:16], ss.ap[0:tp, 14:15], 1.0 / D, 4.0 * EPS, [("ss2", 2)], [("rstd2",)],
                  rsq_tmp[0:tp, 0:1])
            for half in range(2):
                py = PS[4 + (2 * i + half) % 4]
                t_ = tmpD[half]
                stt(t_.ap[0:tp, 0:512], py.ap[0:tp, :], rstd.ap[0:tp, 15:16], gp.ap[0:tp, half * 512:(half + 1) * 512],
                    ALU.mult, ALU.mult, [py, ("rstd2",), gp], [t_])
                tt(R_t[0:tp, i, half * 512:(half + 1) * 512], R_t[0:tp, i, half * 512:(half + 1) * 512],
                   t_.ap[0:tp, 0:512], ALU.add, [R[i], t_], [R[i]], eng="pool")
        if seg == 0 and stage >= 3:
            ffn_load(w["w2g"], w["w2u"], w["w2d"], 0, 0)

    def derived_constants():
        for k in ("n1post", "n2post"):
            ts(gpost[k].ap[:], gpost[k].ap[:], 0.5, None, ALU.mult, None, [gpost[k]], [gpost[k]])
        ts(hgnb.ap[:], hgnb.ap[:], 0.5, None, ALU.mult, None, [hgnb], [hgnb])
        tt(cc.ap[:], lbt.ap[:, 0, :], lbt.ap[:, 1, :], ALU.subtract, [lbt], [cc])
        act(cc.ap[:], cc.ap[:], AF.Tanh, [cc], [cc], scale=0.5)
        ts(ccn.ap[:], cc.ap[:], 0.25, -0.25, ALU.mult, ALU.add, [cc], [ccn])
        ts(cc.ap[:], cc.ap[:], -0.25, 0.25, ALU.mult, ALU.add, [cc], [cc])
        ts(cc1.ap[:], cc.ap[:], -1.0, 1.0, ALU.mult, ALU.add, [cc], [cc1])
        tt(ccr.ap[:], lbr.ap[:, 0, :], lbr.ap[:, 1, :], ALU.subtract, [lbr], [ccr])
        act(ccr.ap[:], ccr.ap[:], AF.Tanh, [ccr], [ccr], scale=0.5)
        ts(ccrn.ap[:], ccr.ap[:], 0.25, -0.25, ALU.mult, ALU.add, [ccr], [ccrn])
        ts(ccr.ap[:], ccr.ap[:], -0.25, 0.25, ALU.mult, ALU.add, [ccr], [ccr])

    selc = Buf(sb("selc", [128, 1]), ["selc"])
    S.dma("sp", [(selc.ap[:], vec[10:11, 0:1].broadcast_to([128, 1]))], [], [selc], "csel")
    PRE[0] = True
    winv_ = w["win"].rearrange("(k p) n -> p k n", p=128)
    S.dma("pool", [(wpre.ap[:, :, :], winv_[:, :, 1280:2304])], [], [wpre], "wpre")

    def x1rows(seg, i, tp, c0):
        return x1s[seg * TS + c0: seg * TS + c0 + 128, :] if tp == 128 else x1s[NSEG * TS: NSEG * TS + NSAMP, :]

    for pseg in range(NSEG):
        for (i, tp, c0) in tiles_of(pseg):
            if tp == 128:
                if pseg > 0:
                    S.dma("sp", [(R[i].ap[:, :], xp[pseg * TS + c0: pseg * TS + c0 + 128, :])], [], [R[i]], "ldx%d" % (i % 4))
        ffn(pseg, w["w1g"], w["w1u"], w["w1d"], "n1pre", "n1post", final=False, s0=1, preloaded=True,
            after_norm=(derived_constants if pseg == 0 else None))
        for (i, tp, c0) in tiles_of(pseg):
            S.dma("sp", [(x1rows(pseg, i, tp, c0), R[i].ap[0:tp, :])], [R[i]], [("x1s", pseg, i)], "spill%d" % (i % 5))
        norm_to_hT(pseg, "nmpre")
        S.dma("sp", [(spH[pseg], hT_t[:].rearrange("p k n -> p (k n)"))], [hT], [("sph", pseg)], "st_sph")
        if pseg == NSEG - 1:
            kv_tail()
        else:
            ffn_load(w["w1g"], w["w1u"], w["w1d"], 0, 1)
        hgrn(pseg, state_only=True)
    PRE[0] = False
    S.phase = "exchange"
    S.dma("pool", [(wpre.ap[:, :, 0:768], winv_[:, :, 0:768])], [], [wpre], "wpre")
    S.dma("pool", [(xsend[:, 0:512], Sf.ap[:].rearrange("p h v -> p (h v)")),
                   (xsend[:, 512:896], tmpD[1].ap[:, 0:384])], [("Sf", h) for h in range(4)] + [tmpD[1]], ["xsend"], "xsend")
    S.collective("AllGather", [[0, 1], [2, 3], [4, 5], [6, 7]], xsend.opt(), xrecv.opt(), ["xsend"], ["xrecv"])
    S.dma("pool", [(tmpD[0].ap[:, 0:896], xrecv[0:128, 0:896])], ["xrecv"], [tmpD[0]], "xrecv")
    def consume_exchange():
        ts(Sf.ap[:].rearrange("p h v -> p (h v)"), tmpD[0].ap[:, 0:512], selc.ap[:, 0:1], None, ALU.mult, None,
           [tmpD[0], selc] + [("Sf", h) for h in range(4)], [("Sf", h) for h in range(4)])
        S.op("act", lambda e: e.copy(out=SB[par[0]].ap[:], in_=Sf.ap[:]), [("Sf", h) for h in range(4)], [SB[par[0]]])
        ts(vtok[0].ap[:], tmpD[0].ap[:, 512:640], selc.ap[:, 0:1], None, ALU.mult, None, [tmpD[0], selc], [vtok[0]])
        ts(kT.ap[:, :, 0:128], tmpD[0].ap[0:64, 640:896].rearrange("p (j n) -> p j n", j=2), selc.ap[0:64, 0:1], None,
           ALU.mult, None, [tmpD[0], selc], [kT])

    for seg in range(NSEG):
        S.phase = "M%d:reload" % seg
        S.dma("sp", [(hT_t[:].rearrange("p k n -> p (k n)"), spH[seg])], [("sph", seg)], [hT], "ld_sph")
        for (i, tp, c0) in tiles_of(seg):
            S.dma("sp", [(R[i].ap[0:tp, :], x1rows(seg, i, tp, c0))], [("x1s", seg, i)], [R[i]], "ldx%d" % (i % 5))
        if seg == 0:
            consume_exchange()
        attention(seg)
        hgrn(seg)
        mixout(seg)
        ffn(seg, w["w2g"], w["w2u"], w["w2d"], "n2pre", "n2post", final=True, s0=0, preloaded=True)
    S.finish()
    es.close()
    if os.environ.get("K_PHASEMAP"):
        import json
        json.dump(S.pmap, open(os.environ["K_PHASEMAP"], "w"))
    return nc


def _consts():
    cm = np.zeros((128, 800), np.float32)
    cm[:, 0:128] = np.eye(128, dtype=np.float32)
    s = np.arange(128)[:, None]
    t = np.arange(128)[None, :]
    cm[:, 128:256] = ((s <= t) & (s // 64 == t // 64)).astype(np.float32)
    i = np.arange(128)[:, None]
    j = np.arange(256)[None, :]
    rel = i + 128 - j
    valid = (rel >= 0) & (rel < 128)
    cm[:, 256:512] = np.where(valid, 0.0, NEG)
    cm[:, 512:768] = np.where(valid & (j >= 128), 0.0, NEG)
    cm[0:16, 768:784] = np.eye(16, dtype=np.float32)
    p = np.arange(64)[:, None]
    cm[0:64, 784:800] = ((p % 16) == np.arange(16)[None, :])
    return cm


_NC_CACHE = {}


def kernel(x_prompt, x_sample, cache_k, cache_v, state_hgrn,
           norm_ffn1_pre, norm_ffn1_post, w_ffn1_gate, w_ffn1_up, w_ffn1_down,
           norm_mix_pre, norm_mix_post, w_in, attn_sinks, hgrn_lb_logits, hgrn_norm,
           w_attn_out, w_hgrn_out, w_out,
           norm_ffn2_pre, norm_ffn2_post, w_ffn2_gate, w_ffn2_up, w_ffn2_down):
    stage = int(os.environ.get("K_STAGE", "99"))
    f = lambda a: np.ascontiguousarray(np.asarray(a, dtype=np.float32))
    vec = np.zeros((11, D), np.float32)
    vec[0] = f(norm_ffn1_pre)[0]
    vec[1] = f(norm_ffn1_post)[0]
    vec[2] = f(norm_mix_pre)[0]
    vec[3] = f(norm_mix_post)[0]
    vec[4] = f(norm_ffn2_pre)[0]
    vec[5] = f(norm_ffn2_post)[0]
    vec[6, 0:8] = f(attn_sinks)[0]
    vec[7, 0:512] = f(hgrn_lb_logits)[0]
    vec[8, 0:512] = f(hgrn_lb_logits)[1]
    vec[9, 0:128] = f(hgrn_norm)[0]
    shared = {
        "w1g": f(w_ffn1_gate)[0], "w1u": f(w_ffn1_up)[0], "w1d": f(w_ffn1_down)[0], "win": f(w_in)[0],
        "wao": f(w_attn_out)[0], "who": f(w_hgrn_out)[0], "wout": f(w_out)[0],
        "w2g": f(w_ffn2_gate)[0], "w2u": f(w_ffn2_up)[0], "w2d": f(w_ffn2_down)[0],
        "vec": vec,
    }
    xpf, xsf = f(x_prompt), f(x_sample)
    ckf, cvf, stf = f(cache_k), f(cache_v), f(state_hgrn)
    in_maps = []
    cm_first = _consts()
    cm_second = cm_first.copy()
    cm_second[:, 512:768] = cm_first[:, 256:512]
    HALF = NSEG * TS
    for c in range(8):
        b, half = c // 2, c % 2
        m = dict(shared)
        m["cmat"] = cm_second if half else cm_first
        vc = vec.copy()
        vc[10, 0] = float(half)
        m["vec"] = vc
        m["xp"] = xpf[b][half * HALF:(half + 1) * HALF]
        m["xs"] = xsf[16 * c:16 * c + 16, 0, :]
        m["ck"] = ckf[0, 16 * c:16 * c + 16].reshape(16, 128, 128)
        m["cv"] = cvf[0, 16 * c:16 * c + 16].reshape(16, 128, 128)
        m["st"] = stf[0, 16 * c:16 * c + 16]
        in_maps.append(m)
    if stage not in _NC_CACHE:
        _NC_CACHE[stage] = build(stage)
    nc = _NC_CACHE[stage]
    in_maps = [{k: v for k, v in m.items() if k in _DECL} for m in in_maps]
    res = run_bass_kernel_spmd(nc, in_maps, core_ids=list(range(8))).results
    y_prompt = np.stack([np.concatenate([res[2 * b]["yp"], res[2 * b + 1]["yp"]], 0) for b in range(4)], 0)
    y_sample = np.concatenate([res[c]["ys"] for c in range(8)], 0)[:, None, :]
    nkp = np.stack([res[2 * b + 1]["nkp"].reshape(128, 2, 64) for b in range(4)], 0)[None]
    nvp = np.stack([res[2 * b + 1]["nvp"].reshape(128, 2, 64) for b in range(4)], 0)[None]
    nhp = np.stack([res[2 * b + 1]["nhp"] for b in range(4)], 0)[None]
    nks = np.concatenate([res[c]["nks"].reshape(16, 128, 2, 64) for c in range(8)], 0)[None]
    nvs = np.concatenate([res[c]["nvs"].reshape(16, 128, 2, 64) for c in range(8)], 0)[None]
    nhs = np.concatenate([res[c]["nhs"] for c in range(8)], 0)[None]
    return (y_prompt.astype(np.float32), y_sample.astype(np.float32), nkp.astype(np.float32),
            nvp.astype(np.float32), nhp.astype(np.float32), nks.astype(np.float32), nvs.astype(np.float32),
            nhs.astype(np.float32))
```

```python
import os
from contextlib import ExitStack

import numpy as np
import concourse.bass as bass
import concourse.mybir as mybir
from concourse.bass_utils import run_bass_kernel_spmd

F32 = mybir.dt.float32
BF16 = mybir.dt.bfloat16
AF = mybir.ActivationFunctionType
ALU = mybir.AluOpType
AX = mybir.AxisListType

D = 1024
FF = 2816
NSEG = int(os.environ.get("K_NSEG", "4"))
TS = 512
NT = 4
NSAMP = 16
EPS = 1e-6
FEAT = "skq2"
NEG = -30000.0
GROUPS = [(0, 4), (4, 4), (8, 4), (12, 4), (16, 3), (19, 3)]


_DECL = []


class Buf:
    def __init__(self, ap, keys):
        self.ap = ap
        self.keys = list(keys)


class Sched:
    def __init__(self, nc, es):
        self.nc, self.es = nc, es
        self.eng = dict(pe=nc.tensor, act=nc.scalar, dve=nc.vector, pool=nc.gpsimd, sp=nc.sync)
        self.csem = {e: es.enter_context(nc.semaphore("c_" + e)) for e in ("pe", "act", "dve", "pool")}
        self.cnt = {e: 0 for e in self.csem}
        self.waited = {e: {} for e in self.eng}
        self.res = {}
        self.dsem = {}
        self.sems = {}
        self.phase = "init"
        self.pmap = {}

    @staticmethod
    def _keys(bufs):
        out = []
        for b in bufs:
            if isinstance(b, Buf):
                out.extend(b.keys)
            elif isinstance(b, list):
                out.extend(Sched._keys(b))
            else:
                out.append(b)
        return out

    def _deps(self, rk, wk):
        toks = []
        for k in rk:
            st = self.res.get(k)
            if st and st[0]:
                toks.append(st[0])
        for k in wk:
            st = self.res.get(k)
            if st:
                if st[0]:
                    toks.append(st[0])
                toks.extend(st[1].values())
        return toks

    def _waits(self, e, toks):
        best = {}
        for (sid, val, src) in toks:
            if src == "pe" and e == "pe":
                continue
            if val > best.get(sid, 0):
                best[sid] = val
        for sid, val in best.items():
            if self.waited[e].get(sid, 0) >= val:
                continue
            self.eng[e].wait_ge(self.sems[sid], val)
            self.waited[e][sid] = val

    def _record(self, tok, rk, wk):
        wk = set(wk)
        for k in wk:
            self.res[k] = [tok, {}]
        for k in rk:
            if k in wk:
                continue
            st = self.res.setdefault(k, [None, {}])
            st[1][tok[0]] = tok

    def op(self, e, fn, reads=(), writes=()):
        rk, wk = self._keys(reads), self._keys(writes)
        self._waits(e, self._deps(rk, wk))
        ins = fn(self.eng[e])
        try:
            self.pmap[ins.ins.name] = self.phase
        except Exception:
            pass
        self.cnt[e] += 1
        ins.then_inc(self.csem[e], 1)
        sid = "c_" + e
        self.sems[sid] = self.csem[e]
        self._record((sid, self.cnt[e], e), rk, wk)

    def dma(self, q, pairs, reads, writes, sem):
        rk, wk = self._keys(reads), self._keys(writes)
        self._waits(q, self._deps(rk, wk))
        if sem not in self.dsem:
            self.dsem[sem] = [self.es.enter_context(self.nc.semaphore("d_" + sem)), 0]
            self.sems["d_" + sem] = self.dsem[sem][0]
        s = self.dsem[sem]
        for (o, i) in pairs:
            self.eng[q].dma_start(out=o, in_=i).then_inc(s[0], 16)
            s[1] += 16
        self._record(("d_" + sem, s[1], "dma"), rk, wk)

    def collective(self, kind, groups, src, dst, reads, writes):
        rk, wk = self._keys(reads), self._keys(writes)
        self._waits("pool", self._deps(rk, wk))
        sem = self.es.enter_context(self.nc.semaphore("cc_sem"))
        self.sems["cc_sem"] = sem
        self.nc.gpsimd.collective_compute(kind, mybir.AluOpType.bypass, replica_groups=groups, ins=[src],
                                          outs=[dst]).then_inc(sem)
        self._record(("cc_sem", 1, "cc"), rk, wk)

    def finish(self):
        for name, (sem, cnt) in self.dsem.items():
            if cnt:
                self.nc.sync.wait_ge(sem, cnt)


def build(stage=99):
    nc = bass.Bass("TRN2", target_bir_lowering=False)
    es = ExitStack()
    es.enter_context(nc.allow_non_contiguous_dma(reason="small strided constant loads"))
    S = Sched(nc, es)

    _DECL.clear()

    def din(name, shape):
        _DECL.append(name)
        return nc.dram_tensor(name, list(shape), F32, kind="ExternalInput").ap()

    def dout(name, shape):
        return nc.dram_tensor(name, list(shape), F32, kind="ExternalOutput").ap()

    xp = din("xp", [NSEG * TS, D])
    xs = din("xs", [NSAMP, D])
    ck = din("ck", [NSAMP, 128, 128])
    cv = din("cv", [NSAMP, 128, 128])
    st = din("st", [NSAMP, 4, 128, 128])
    w = {}
    for n_, sh in [("w1g", [D, FF]), ("w1u", [D, FF]), ("w1d", [FF, D]), ("win", [D, 4864]),
                   ("wao", [512, D]), ("who", [512, D]), ("wout", [D, D]),
                   ("w2g", [D, FF]), ("w2u", [D, FF]), ("w2d", [FF, D])]:
        w[n_] = din(n_, sh)
    vec = din("vec", [11, D])
    cmat = din("cmat", [128, 800])
    VROW = {"n1pre": 0, "n1post": 1, "nmpre": 2, "nmpost": 3, "n2pre": 4, "n2post": 5, "sinks": 6, "lbl0": 7,
            "lbl1": 8, "hgn": 9}
    for k_, r_ in VROW.items():
        w[k_] = vec[r_:r_ + 1, :]
    w["c_ident"] = cmat[:, 0:128]
    w["c_tri"] = cmat[:, 128:256]
    w["c_mask"] = cmat[:, 256:512]
    w["c_mask0"] = cmat[:, 512:768]
    w["c_id16"] = cmat[0:16, 768:784]
    w["c_dm"] = cmat[0:64, 784:800]
    yp = dout("yp", [NSEG * TS, D])
    ys = dout("ys", [NSAMP, D])
    nkp = dout("nkp", [128, 128])
    nvp = dout("nvp", [128, 128])
    nhp = dout("nhp", [4, 128, 128])
    nks = dout("nks", [NSAMP, 128, 128])
    nvs = dout("nvs", [NSAMP, 128, 128])
    nhs = dout("nhs", [NSAMP, 4, 128, 128])
    x1s = nc.dram_tensor("x1s", [NSEG * TS + NSAMP, D], F32).ap()
    xsend = nc.dram_tensor("xsend", [128, 1024], F32).ap()
    spH = nc.dram_tensor("spH", [NSEG, 128, 8 * (TS + NSAMP)], BF16).ap()
    spP = nc.dram_tensor("spP", [NSEG, 128, 2048], F32).ap()
    spK = nc.dram_tensor("spK", [NSEG, 128, 4096], BF16).ap()
    spK2 = nc.dram_tensor("spK2", [NSEG, 128, 1024], BF16).ap()
    spPt = nc.dram_tensor("spPt", [NSEG, 128, 4 * NT], F32).ap()
    xrecv = nc.dram_tensor("xrecv", [256, 1024], F32).ap()

    def sb(name, shape, dt=F32):
        return es.enter_context(nc.sbuf_tensor(name, list(shape), dt))

    NTT = NT + 1
    NC = TS + NSAMP
    GR = 512
    R_t = sb("R", [128, NTT, D])
    hT_t = sb("hT", [128, 8, NC], BF16)
    AR_t = sb("arena", [128, 38912], BF16)
    oaT_t = sb("oaT", [128, 4, NC], BF16)
    ohT_t = sb("ohT", [128, 4, NC], BF16)
    R = [Buf(R_t[:, i, :], [("R", i)]) for i in range(NTT)]
    hT = Buf(hT_t, ["hT"])
    oaT = Buf(oaT_t, ["oaT"])
    ohT = Buf(ohT_t, ["ohT"])

    def akeys(off, n):
        return [("AR", g) for g in range(off // GR, (off + n - 1) // GR + 1)]

    def arena(off, n, shape_str=None, dt=BF16, **kw):
        ap = AR_t[:, off:off + n]
        if dt == F32:
            ap = ap.bitcast(F32)
        if shape_str:
            ap = ap.rearrange(shape_str, **kw)
        b = Buf(ap, akeys(off, n))
        b.off = off
        return b

    Y_b = arena(0, 2 * NTT * D, "p (i n) -> p i n", dt=F32, i=NTT)
    Y_t = Y_b.ap

    def ykeys(i, half=None):
        if half is None:
            return akeys(i * 2 * D, 2 * D)
        return akeys(i * 2 * D + half * D, D)
    UT0 = 22528
    FSLOT = [10240, 26624]
    uT = [arena(UT0 + i * 2048, 2048, "p (j n) -> p j n", j=4) for i in range(2)]
    TA = 16384
    qT = arena(TA, 4096, "p (g n) -> p g n", g=8)
    smk = arena(TA + 4096, 4096, "p (h k) -> p h k", dt=F32, h=8)
    sk = arena(TA + 4096, 4224, "p (h k) -> p h k", dt=F32, h=8)
    pbf = arena(TA + 8320, 2176, "p (h k) -> p h k", h=8)
    pTs = arena(TA + 10496, 2048, "p (b n) -> p b n", b=16)
    KTs = arena(TA + 12544, 2048)
    Ks = arena(TA + 14592, 2048, "p (b n) -> p b n", b=16)
    Vsj = [arena(TA + 16640 + j * 1024, 1024, "p (b n) -> p b n", b=16) for j in range(2)]
    pbf2 = arena(TA + 18688, 2176, "p (h k) -> p h k", h=8)
    pbfs = [pbf, pbf2]

    def skk(b4):
        return akeys(sk.off + b4 * 1056, 1056)
    THG = 16384
    Pm = [arena(THG + h * 1024, 1024, dt=F32) for h in range(4)]
    qeT = [arena(THG + 4096 + h * 512, 512) for h in range(4)]
    keT = [arena(THG + 6144 + h * 512, 512) for h in range(4)]
    kdT = [arena(THG + 8192 + h * 512, 512) for h in range(4)]
    smk2 = arena(THG + 10240, 4096, "p (h k) -> p h k", dt=F32, h=8)
    Ssg = arena(THG + 14336, 4096, "p (h b v) -> p h b v", dt=F32, h=4, b=4)
    Ssg2 = arena(THG + 4096, 4096, "p (h b v) -> p h b v", dt=F32, h=4, b=4)
    vexp = [arena(THG + 18432 + i * 2048, 2048, "p (b v) -> p b v", b=16) for i in range(2)] + \
           [arena(THG + i * 2048, 2048, "p (b v) -> p b v", b=16) for i in range(2)]
    mT0 = arena(6144, 8 * NC, "p (m c) -> p m c", m=8)
    mT1 = arena(6144, 8 * TS, "p (m c) -> p m c", m=8)

    def smkk(buf, b4):
        return akeys(buf.off + b4 * 2048, 2048)

    ident = Buf(sb("ident", [128, 128], BF16), ["ident"])
    tri = Buf(sb("tri", [128, 128]), ["tri"])
    maskN = Buf(sb("maskN", [128, 256]), ["maskN"])
    mask0 = Buf(sb("mask0", [128, 256]), ["mask0"])
    id16 = Buf(sb("id16", [16, 16]), ["id16"])
    dm = Buf(sb("dm", [64, 16]), ["dm"])
    gpre = {k: Buf(sb("g_" + k, [128, 8]), ["g_" + k]) for k in ("n1pre", "nmpre", "n2pre")}
    gpost = {k: Buf(sb("g_" + k, [128, D]), ["g_" + k]) for k in ("n1post", "nmpost", "n2post")}
    hgnb = Buf(sb("hgnb", [128, 128]), ["hgnb"])
    sinkb = Buf(sb("sinkb", [128, 8]), ["sinkb"])
    sinkc = Buf(sb("sinkc", [64, 2]), ["sinkc"])
    lbt = Buf(sb("lbt", [128, 2, 4]), ["lbt"])
    cc = Buf(sb("cc", [128, 4]), ["cc"])
    ccn = Buf(sb("ccn", [128, 4]), ["ccn"])
    cc1 = Buf(sb("cc1", [128, 4]), ["cc1"])
    lbr = Buf(sb("lbr", [16, 2, 512]), ["lbr"])
    ccr = Buf(sb("ccr", [16, 512]), ["ccr"])
    ccrn = Buf(sb("ccrn", [16, 512]), ["ccrn"])
    zer = Buf(sb("zer", [128, 64]), ["zer"])
    d1z = Buf(sb("d1z", [128, 512]), ["d1z"])
    nhalf = Buf(sb("nhalf", [128, 16]), ["nhalf"])
    ss = Buf(sb("ss", [128, 16]), ["ss"])
    rstd = Buf(sb("rstd", [128, 16]), ["rstd"])
    junk = Buf(sb("junk", [128, D], BF16), ["junk"])
    hb = [Buf(sb("hb%d" % i, [128, D], BF16), ["hb%d" % i]) for i in range(2)]
    tmpA = [Buf(sb("tmpA%d" % i, [128, 512]), ["tmpA%d" % i]) for i in range(2)]
    tmpB = [Buf(sb("tmpB%d" % i, [128, 512]), ["tmpB%d" % i]) for i in range(2)]
    tmpC = [Buf(sb("tmpC%d" % i, [128, 512]), ["tmpC%d" % i]) for i in range(2)]
    tmpD = [Buf(sb("tmpD%d" % i, [128, D]), ["tmpD%d" % i]) for i in range(2)]
    kT = Buf(sb("kT", [64, 2, 128 + NC], BF16), ["kT"])
    vtok = [Buf(sb("vtok%d" % i, [128, 128], BF16), [("vtok", i)]) for i in range(NT + 1)]
    kvf = Buf(sb("kvf", [128, 256]), ["kvf"])
    st8 = {k: Buf(sb("st8" + k, [128, 8]), ["st8" + k]) for k in ("mx", "nmx", "sum", "es", "rinv")}
    st8b = {k: Buf(sb("st8b" + k, [128, 8]), ["st8b" + k]) for k in ("mx", "nmx", "sum", "es", "rinv")}
    st8s = [st8, st8b]
    obf = Buf(sb("obf", [128, 512], BF16), ["obf"])
    Sf = Buf(sb("Sf", [128, 4, 128]), ["Sf"])
    Sb16 = Buf(sb("Sb16", [128, 4, 128], BF16), ["Sb16"])
    Sb16m = Buf(sb("Sb16m", [128, 4, 128], BF16), ["Sb16m"])
    SB = [Sb16, Sb16m]
    par = [0]
    Pt = Buf(sb("Pt", [128, 4, NT]), ["Pt"])
    kd2c0 = Buf(sb("kd2c0", [128, 4, NT * 64], BF16), ["kd2c0"])
    qe2c1 = Buf(sb("qe2c1", [128, 4, NT * 64], BF16), ["qe2c1"])
    wpre = Buf(sb("wpre", [128, 8, 1024], BF16), ["wpre"])
    kdtok = Buf(sb("kdtok", [128, 4, 128], BF16), ["kdtok"])
    vh = Buf(sb("vh", [128, 512], BF16), ["vh"])
    sgg = Buf(sb("sgg", [128, 512]), ["sgg"])
    ATs = Buf(sb("ATs", [128, 4, 128], BF16), ["ATs"])
    ohn = Buf(sb("ohn", [128, 512], BF16), ["ohn"])
    ssh = Buf(sb("ssh", [128, 4]), ["ssh"])
    rsh = Buf(sb("rsh", [128, 4]), ["rsh"])
    qs = Buf(sb("qs", [64, 8, 16], BF16), ["qs"])
    s_s = Buf(sb("s_s", [64, 128]), ["s_s"])
    p_s = Buf(sb("p_s", [64, 128], BF16), ["p_s"])
    pT_s = Buf(sb("pT_s", [128, 64], BF16), ["pT_s"])
    o_s = Buf(sb("o_s", [64, 64]), ["o_s"])
    o_sb = Buf(sb("o_sb", [64, 64], BF16), ["o_sb"])
    oTs = Buf(sb("oTs", [64, 8, 16], BF16), ["oTs"])
    st1 = {k: Buf(sb("st1" + k, [64, 1]), ["st1" + k]) for k in ("mx", "nmx", "sum", "es", "rinv")}
    khtok = Buf(sb("khtok", [16, 512], BF16), ["khtok"])
    fvs = Buf(sb("fvs", [128, 4, 16]), ["fvs"])
    qss = Buf(sb("qss", [128, 4, 16]), ["qss"])
    o_hs = Buf(sb("o_hs", [16, 512]), ["o_hs"])

    PS = [Buf(es.enter_context(nc.psum_tensor("ps%d" % i, [128, 512], F32)), [("ps", i)]) for i in range(8)]

    def psb(i):
        return PS[i].ap[:].bitcast(BF16)

    def act(out, in_, func, reads, writes, **kw):
        S.op("act", lambda e: e.activation(out=out, in_=in_, func=func, **kw), reads, writes)

    def tt(out, a, b, op, reads, writes, eng="dve"):
        S.op(eng, lambda e: e.tensor_tensor(out=out, in0=a, in1=b, op=op), reads, writes)

    def ts(out, a, s1, s2, op0, op1, reads, writes, eng="dve"):
        if s2 is None:
            S.op(eng, lambda e: e.tensor_scalar(out=out, in0=a, scalar1=s1, scalar2=None, op0=op0), reads, writes)
        else:
            S.op(eng, lambda e: e.tensor_scalar(out=out, in0=a, scalar1=s1, scalar2=s2, op0=op0, op1=op1),
                 reads, writes)

    def stt(out, a, s, b, op0, op1, reads, writes):
        S.op("dve", lambda e: e.scalar_tensor_tensor(out=out, in0=a, scalar=s, in1=b, op0=op0, op1=op1),
             reads, writes)

    def cp(out, in_, reads, writes, eng="dve"):
        S.op(eng, lambda e: e.tensor_copy(out=out, in_=in_), reads, writes)

    def mm(out, pairs, reads, writes):
        def f(e):
            n = len(pairs)
            for i, (l, r) in enumerate(pairs):
                ins = e.matmul(out, lhsT=l, rhs=r, start=(i == 0), stop=(i == n - 1))
            return ins
        S.op("pe", f, reads, writes)

    def rsqrt(out, in_, scale, eps, reads, writes, tmp, key="rsq_tmp"):
        ts(tmp, in_, scale, eps, ALU.mult, ALU.add, reads, [key])
        S.op("pool", lambda e: e.tensor_tensor(out=out, in0=tmp, in1=nhalf.ap[0:tmp.shape[0], 0:tmp.shape[1]],
                                               op=ALU.pow), [key, nhalf], writes)

    rsq_tmp = sb("rsq_tmp", [128, 16])
    rsq_tmpn = sb("rsq_tmpn", [128, 8])
    rstdn = sb("rstdn", [128, 8])

    def early_ffn_load():
        h0, G = GROUPS[0]
        o = 26624
        a = arena(o, 4096, "p (k n) -> p k n", k=8)
        b = arena(o + 4096, 4096, "p (k n) -> p k n", k=8)
        c = arena(o + 8192, 4096, "p (k n) -> p k n", k=4)
        S.dma("pool", [(a.ap[:, :, 0:G * 128], w["w1g"].rearrange("(k p) n -> p k n", p=128)[:, :, h0 * 128:(h0 + G) * 128]),
                       (b.ap[:, :, 0:G * 128], w["w1u"].rearrange("(k p) n -> p k n", p=128)[:, :, h0 * 128:(h0 + G) * 128]),
                       (c.ap[:, 0:G, :], w["w1d"].rearrange("(k p) n -> p k n", p=128)[:, h0:h0 + G, :])], [], [a, b, c], "wf1")

    def ld(q, buf, src, sem):
        S.dma(q, [(buf.ap[:] if not isinstance(buf.ap, bass.AP) else buf.ap, src)], [], [buf], sem)

    if True:
        ld("pool", ident, w["c_ident"], "c0")
        for (i_, tp_, c0_) in [(i, 128, i * 128) for i in range(NT)]:
            S.dma("sp", [(R[i_].ap[:, :], xp[c0_: c0_ + 128, :])], [], [R[i_]], "ldx%d" % (i_ % 4))
        early_ffn_load()
        S.op("dve", lambda e: e.memset(R[NT].ap[:, :], 0.0), [], [R[NT]])
        S.dma("sp", [(R[NT].ap[0:NSAMP, :], xs[:, :])], [], [R[NT]], "ldxs")
        S.dma("sp", [(gpre["n1pre"].ap[:], w["n1pre"].rearrange("o (k p) -> p (o k)", p=128))], [], [gpre["n1pre"]], "c60")
        ld("sp", tri, w["c_tri"], "c1")
        ld("sp", maskN, w["c_mask"], "c2")
        ld("sp", mask0, w["c_mask0"], "c3")
        ld("sp", id16, w["c_id16"], "c4")
        ld("sp", dm, w["c_dm"], "c5")
        for i_, k in enumerate(("n1pre", "nmpre", "n2pre")):
            if i_ > 0:
                S.dma("sp", [(gpre[k].ap[:], w[k].rearrange("o (k p) -> p (o k)", p=128))], [], [gpre[k]], "c6%d" % i_)
        for i_, k in enumerate(("n1post", "nmpost", "n2post")):
            S.dma("sp", [(gpost[k].ap[:], w[k].broadcast_to([128, D]))], [], [gpost[k]], "c7%d" % i_)
        S.dma("sp", [(hgnb.ap[:], w["hgn"][:, 0:128].broadcast_to([128, 128]))], [], [hgnb], "c8")
        S.dma("sp", [(sinkb.ap[:], w["sinks"][:, 0:8].broadcast_to([128, 8]))], [], [sinkb], "c9")
        S.dma("sp", [(sinkc.ap[g * 16:(g + 1) * 16, j:j + 1],
                      w["sinks"][0:1, 4 * j + g:4 * j + g + 1].broadcast_to([16, 1]))
                     for j in range(2) for g in range(4)], [], [sinkc], "ca")
        S.dma("sp", [(lbt.ap[:, r_, :], w["lbl%d" % r_][:, 0:512].rearrange("o (h p) -> p (o h)", p=128))
                     for r_ in range(2)], [], [lbt], "cb")
        S.dma("sp", [(lbr.ap[:, r_, :], w["lbl%d" % r_][:, 0:512].broadcast_to([16, 512])) for r_ in range(2)],
              [], [lbr], "cc")
        S.op("dve", lambda e: e.memset(zer.ap[:], 0.0), [], [zer])
        S.op("dve", lambda e: e.memset(d1z.ap[:], 0.0), [], [d1z])
        S.op("dve", lambda e: e.memset(nhalf.ap[:], -0.5), [], [nhalf])
        S.op("dve", lambda e: e.memset(Sf.ap[:], 0.0), [], [Sf] + [("Sf", h_) for h_ in range(4)])
        S.op("dve", lambda e: e.memset(Sb16.ap[:], 0.0), [], [Sb16])
        S.op("dve", lambda e: e.memset(Sb16m.ap[:], 0.0), [], [Sb16m])
        S.op("dve", lambda e: e.memset(kT.ap[:, :, 0:128], 0.0), [], [kT])
        S.op("dve", lambda e: e.memset(vtok[0].ap[:], 0.0), [], [vtok[0]])
        S.op("dve", lambda e: e.memset(ss.ap[:], 1.0), [], [ss])
    S.dma("sp", [(nks[:, 0:127, :], ck[:, 1:128, :])], [], ["nks_d"], "o_nks")
    S.dma("sp", [(nvs[:, 0:127, :], cv[:, 1:128, :])], [], ["nvs_d"], "o_nvs")

    PRE = [False]

    def tiles_of(seg):
        t = [(i, 128, i * 128) for i in range(NT)]
        if seg == 0:
            t.append((NT, NSAMP, TS))
        return t

    def blocks_of(seg):
        b = [(0, TS)]
        if seg == 0:
            b.append((TS, TS + NSAMP))
        return b

    def norm_to_hT(seg, gkey):
        S.phase = "%s%d:norm_%s" % ("P" if PRE[0] else "M", seg, gkey)
        tl = tiles_of(seg)
        g = gpre[gkey]

        def sq(i, tp, c0):
            act(junk.ap[0:tp, :], R[i].ap[0:tp, :], AF.Square, [R[i], ss], [junk, ("ssn", i)],
                accum_out=ss.ap[0:tp, 5 + i:6 + i])
            rsqrt(rstdn[0:tp, i:i + 1], ss.ap[0:tp, 5 + i:6 + i], 1.0 / D, EPS, [("ssn", i)], [("rstdn", i)],
                  rsq_tmpn[0:tp, i:i + 1], key=("rsqn", i))

        def fin(i, tp, c0):
            h_ = hb[i % 2]
            act(h_.ap[0:tp, :], R[i].ap[0:tp, :], AF.Copy, [R[i], ("rstdn", i)], [h_], scale=rstdn[0:tp, i:i + 1])
            pb = 6 + (i % 2)
            pv = psb(pb).rearrange("p (k n) -> p k n", k=8)

            def f(e, h_=h_, tp=tp, pv=pv):
                for k in range(8):
                    ins = e.transpose(out=pv[:, k, 0:tp], in_=h_.ap[0:tp, k * 128:(k + 1) * 128],
                                      identity=ident.ap[0:tp, 0:tp])
                return ins
            S.op("pe", f, [h_, ident], [PS[pb]])
            tt(hT_t[:, :, c0:c0 + tp], pv[:, :, 0:tp], g.ap[:].unsqueeze(2).broadcast_to([128, 8, tp]),
               ALU.mult, [PS[pb], g], [hT])
        for idx, t_ in enumerate(tl):
            sq(*t_)
            if idx >= 1:
                fin(*tl[idx - 1])
        fin(*tl[-1])

    def ffn_slots():
        sl = []
        for o in FSLOT:
            sl.append((arena(o, 4096, "p (k n) -> p k n", k=8),
                       arena(o + 4096, 4096, "p (k n) -> p k n", k=8),
                       arena(o + 8192, 4096, "p (k n) -> p k n", k=4)))
        return sl

    def ffn_load(wg, wu, wd, gi, si):
        h0, G = GROUPS[gi]
        a, b, c = ffn_slots()[si]
        S.dma("pool", [(a.ap[:, :, 0:G * 128], wg.rearrange("(k p) n -> p k n", p=128)[:, :, h0 * 128:(h0 + G) * 128]),
                       (b.ap[:, :, 0:G * 128], wu.rearrange("(k p) n -> p k n", p=128)[:, :, h0 * 128:(h0 + G) * 128]),
                       (c.ap[:, 0:G, :], wd.rearrange("(k p) n -> p k n", p=128)[:, h0:h0 + G, :])], [], [a, b, c],
              "wf%d" % si)

    def ffn(seg, wg, wu, wd, gpre_k, gpost_k, final, s0=0, preloaded=False, at_last=None, after_norm=None):
        tl = tiles_of(seg)
        bl = blocks_of(seg)
        norm_to_hT(seg, gpre_k)
        if after_norm is not None:
            after_norm()
        S.phase = "%s%d:ffn_%s" % ("P" if PRE[0] else "M", seg, gpre_k)
        wgv = wg.rearrange("(k p) n -> p k n", p=128)
        wuv = wu.rearrange("(k p) n -> p k n", p=128)
        wdv = wd.rearrange("(k p) n -> p k n", p=128)
        slots = ffn_slots()

        def load(gi):
            ffn_load(wg, wu, wd, gi, (gi + s0) % 2)
        if not preloaded:
            load(0)
        ui = 0
        for gi, (h0, G) in enumerate(GROUPS):
            if gi + 1 < len(GROUPS):
                load(gi + 1)
            elif at_last is not None:
                at_last()
            a, b, c = slots[(gi + s0) % 2]
            for (c0, c1) in bl:
                N = c1 - c0
                u = uT[ui % 2]
                ui += 1
                for j in range(G):
                    pg, pu = PS[(2 * j) % 4], PS[(2 * j + 1) % 4]
                    mm(pg.ap[:, 0:N], [(a.ap[:, k, j * 128:(j + 1) * 128], hT_t[:, k, c0:c1]) for k in range(8)],
                       [a, hT], [pg])
                    mm(pu.ap[:, 0:N], [(b.ap[:, k, j * 128:(j + 1) * 128], hT_t[:, k, c0:c1]) for k in range(8)],
                       [b, hT], [pu])
                    t_ = tmpA[j % 2]
                    act(t_.ap[:, 0:N], pg.ap[:, 0:N], AF.Silu, [pg], [t_])
                    tt(u.ap[:, j, 0:N], t_.ap[:, 0:N], pu.ap[:, 0:N], ALU.mult, [t_, pu], [akeys(u.off + j * 512, 512)])
                ukeys = [akeys(u.off, G * 512)]
                for (i, tp, tc0) in tl:
                    if not (c0 <= tc0 < c1):
                        continue
                    off = tc0 - c0
                    for half in range(2):
                        py = PS[4 + (2 * i + half) % 4]
                        mm(py.ap[0:tp, :], [(u.ap[:, j, off:off + tp], c.ap[:, j, half * 512:(half + 1) * 512])
                                            for j in range(G)], ukeys + [c], [py])
                        ysl = Y_t[0:tp, i, half * 512:(half + 1) * 512]
                        if gi == 0:
                            cp(ysl, py.ap[0:tp, :], [py], [ykeys(i, half)])
                        else:
                            tt(ysl, ysl, py.ap[0:tp, :], ALU.add, [py, ykeys(i, half)], [ykeys(i, half)])
        for (i, tp, c0) in tl:
            act(junk.ap[0:tp, :], Y_t[0:tp, i, :], AF.Square, [ykeys(i), ss], [junk, ("ss", i)],
                accum_out=ss.ap[0:tp, i:i + 1])
        n = len(tl)
        rsqrt(rstd.ap[:, 0:n], ss.ap[:, 0:n], 1.0 / D, EPS, [("ss", i) for i in range(n)] + [ss], [rstd],
              rsq_tmp[:, 0:n])
        gp = gpost[gpost_k]
        for (i, tp, c0) in tl:
            t_ = tmpD[i % 2]
            stt(t_.ap[0:tp, :], Y_t[0:tp, i, :], rstd.ap[0:tp, i:i + 1], gp.ap[0:tp, :], ALU.mult, ALU.mult,
                [ykeys(i), rstd, gp], [t_])
            tt(R[i].ap[0:tp, :], R[i].ap[0:tp, :], t_.ap[0:tp, :], ALU.add, [R[i], t_], [R[i]], eng="pool")
            if final:
                if tp == 128:
                    S.dma("sp", [(yp[seg * TS + c0: seg * TS + c0 + 128, :], R[i].ap[:, :])], [R[i]], [],
                          "o_y%d" % (i % 4))
                else:
                    S.dma("sp", [(ys[:, :], R[i].ap[0:tp, :])], [R[i]], [], "o_ys")

    def attention(seg):
        tl = tiles_of(seg)
        wq = wpre
        S.phase = "M%d:attn" % seg
        whg_load(with_f=(seg == 0))
        cp(sk.ap[:, :, 256:257], sinkb.ap[:].unsqueeze(2), [sinkb], [akeys(sk.off, 4224)])
        last_seg = (seg == NSEG - 1)
        for (c0, c1) in [(0, TS)]:
            for h in range(8):
                pq = PS[h % 2]
                mm(pq.ap[0:64, 0:512], [(wq.ap[:, k, h * 64:(h + 1) * 64], hT_t[:, k, c0:c1]) for k in range(8)],
                   [wq, hT], [pq])
                act(qT.ap[0:64, h, :], pq.ap[0:64, 0:512], AF.Copy, [pq], [akeys(qT.off + h * 512, 512)], scale=0.125)
            for j in range(2):
                pk = PS[2]
                mm(pk.ap[0:64, 0:512], [(wq.ap[:, k, 512 + j * 64:512 + (j + 1) * 64], hT_t[:, k, c0:c1])
                                        for k in range(8)], [wq, hT], [pk])
                cp(kT.ap[:, j, 128 + c0:128 + c1], pk.ap[0:64, 0:512], [pk], [kT])
            pv_ = PS[3]
            for (i, tp, tc0) in tl:
                if tp != 128:
                    continue
                mm(pv_.ap[:, i * 128:(i + 1) * 128], [(hT_t[:, k, tc0:tc0 + 128], wq.ap[:, k, 640:768]) for k in range(8)],
                   [wq, hT], [pv_])
                cp(vtok[i + 1].ap[:], pv_.ap[:, i * 128:(i + 1) * 128], [pv_], [vtok[i + 1]])
            if last_seg:
                i, tc0 = NT - 1, (NT - 1) * 128
                cp(kvf.ap[:, 128:256], pv_.ap[:, i * 128:(i + 1) * 128], [pv_], [("kvf", 1)])
                pk2 = PS[2]
                mm(pk2.ap[:, 0:128], [(hT_t[:, k, tc0:tc0 + 128], wq.ap[:, k, 512:640]) for k in range(8)],
                   [wq, hT], [pk2])
                cp(kvf.ap[:, 0:128], pk2.ap[:, 0:128], [pk2], [("kvf", 0)])
                S.dma("sp", [(nkp[:, :], kvf.ap[:, 0:128])], [("kvf", 0)], [], "o_nkp")
                S.dma("sp", [(nvp[:, :], kvf.ap[:, 128:256])], [("kvf", 1)], [], "o_nvp")

            def front(i, tc0):
                off = tc0 - c0
                pb_, st = pbfs[i % 2], st8s[i % 2]
                for h in range(8):
                    pb = PS[4 + h // 2]
                    mm(pb.ap[:, (h % 2) * 256:(h % 2) * 256 + 256],
                       [(qT.ap[0:64, h, off:off + 128], kT.ap[:, h // 4, tc0:tc0 + 256])],
                       [akeys(qT.off + h * 512, 512), kT], [pb])
                mk = mask0 if (seg == 0 and i == 0) else maskN
                for b4 in range(4):
                    pb = PS[4 + b4]
                    tt(sk.ap[:, 2 * b4:2 * b4 + 2, 0:256], pb.ap[:, :].rearrange("p (h k) -> p h k", h=2),
                       mk.ap[:].unsqueeze(1).broadcast_to([128, 2, 256]), ALU.add, [pb, mk], [skk(b4)])
                sk_k = [skk(b4) for b4 in range(4)]
                S.op("dve", lambda e: e.tensor_reduce(out=st["mx"].ap[:], in_=sk.ap[:, :, 0:257], axis=AX.X, op=ALU.max),
                     sk_k, [st["mx"]])
                ts(st["nmx"].ap[:], st["mx"].ap[:], -1.0, None, ALU.mult, None, [st["mx"]], [st["nmx"]])
                for h in range(8):
                    act(pb_.ap[:, h, 0:257], sk.ap[:, h, 0:257], AF.Exp, [skk(h // 2), st["nmx"]],
                        [akeys(pb_.off + h * 272, 272), st["sum"]], bias=st["nmx"].ap[:, h:h + 1],
                        accum_out=st["sum"].ap[:, h:h + 1])

            def back(i, tc0):
                pb_, st = pbfs[i % 2], st8s[i % 2]
                S.op("dve", lambda e: e.reciprocal(out=st["rinv"].ap[:], in_=st["sum"].ap[:]), [st["sum"]], [st["rinv"]])
                for half in range(2):
                    pvw = psb(half).rearrange("p (b n) -> p b n", b=8)

                    def f(e, half=half, pvw=pvw):
                        for q_ in range(8):
                            blk = half * 8 + q_
                            h, kb = blk // 2, blk % 2
                            ins = e.transpose(out=pvw[:, q_, :], in_=pb_.ap[:, h, kb * 128:(kb + 1) * 128],
                                              identity=ident.ap[:])
                        return ins
                    S.op("pe", f, [akeys(pb_.off + half * 1088, 1088), ident], [PS[half]])
                    if half == 0:
                        cp(pTs.ap[:, 0:8, :], pvw, [PS[0]], [akeys(pTs.off, 1024)])
                    else:
                        cp(pTs.ap[:, 8:16, :], pvw, [PS[1]], [akeys(pTs.off + 1024, 1024)])
                po = PS[2]

                def f(e, i=i):
                    for h in range(8):
                        a_ = h // 4
                        for kb in range(2):
                            ins = e.matmul(po.ap[:, h * 64:(h + 1) * 64], lhsT=pTs.ap[:, 2 * h + kb, :],
                                           rhs=vtok[i + kb].ap[:, a_ * 64:(a_ + 1) * 64], start=(kb == 0),
                                           stop=(kb == 1))
                    return ins
                S.op("pe", f, [pTs, vtok[i], vtok[i + 1]], [po])
                tt(obf.ap[:].rearrange("p (h d) -> p h d", h=8), po.ap[:, :].rearrange("p (h d) -> p h d", h=8),
                   st["rinv"].ap[:].unsqueeze(2).broadcast_to([128, 8, 64]), ALU.mult, [po, st["rinv"]], [obf])
                pvw = psb(3).rearrange("p (b n) -> p b n", b=8)

                def f(e, pvw=pvw):
                    for c_ in range(4):
                        ins = e.transpose(out=pvw[:, c_, :], in_=obf.ap[:, c_ * 128:(c_ + 1) * 128],
                                          identity=ident.ap[:])
                    return ins
                S.op("pe", f, [obf, ident], [PS[3]])
                cp(oaT_t[:, :, tc0:tc0 + 128], pvw[:, 0:4, :], [PS[3]], [oaT])

            ptl = [(i, tc0) for (i, tp, tc0) in tl if tp == 128]
            front(*ptl[0])
            for n_, (i, tc0) in enumerate(ptl):
                if n_ + 1 < len(ptl):
                    front(*ptl[n_ + 1])
                back(i, tc0)
        cp(kT.ap[:, :, 0:128], kT.ap[:, :, TS:TS + 128], [kT], [kT])
        cp(vtok[0].ap[:], vtok[NT].ap[:], [vtok[NT]], [vtok[0]])
        if seg == 0:
            sample_attention(wq)

    def sample_attention(wq):
        sc0 = TS
        S.phase = "M0:sattn"
        pq = PS[0]

        def f(e):
            for h in range(8):
                for k in range(8):
                    ins = e.matmul(pq.ap[0:64, h * 16:(h + 1) * 16], lhsT=wq.ap[:, k, h * 64:(h + 1) * 64],
                                   rhs=hT_t[:, k, sc0:sc0 + NSAMP], start=(k == 0), stop=(k == 7))
            return ins
        S.op("pe", f, [wq, hT], [pq])
        act(qs.ap[:].rearrange("p h b -> p (h b)"), pq.ap[0:64, 0:128], AF.Copy, [pq], [qs], scale=0.125)
        pk = PS[1]
        mm(pk.ap[0:NSAMP, 0:256], [(hT_t[:, k, sc0:sc0 + NSAMP], wq.ap[:, k, 512:768]) for k in range(8)],
           [wq, hT], [pk])
        cp(kvf.ap[0:NSAMP, :], pk.ap[0:NSAMP, 0:256], [pk], [("kvf", 0), ("kvf", 1)])
        S.dma("sp", [(nks[:, 127, :], kvf.ap[0:NSAMP, 0:128])], [("kvf", 0)], ["nks_d"], "o_nks2")
        S.dma("sp", [(nvs[:, 127, :], kvf.ap[0:NSAMP, 128:256])], [("kvf", 1)], ["nvs_d"], "o_nvs2")
        S.dma("pool", [(Ks.ap[:], nks.rearrange("b w n -> w b n"))], ["nks_d"], [Ks], "ld_ks")
        S.dma("pool", [(Vsj[j].ap[:], nvs.rearrange("b w (j d) -> w b j d", j=2)[:, :, j, :]) for j in range(2)],
              ["nvs_d"], Vsj, "ld_vs")
        for j in range(2):
            for half in range(2):
                pvw = psb(2 + half).rearrange("p (b n) -> p b n", b=8)

                def f(e, half=half, pvw=pvw, j=j):
                    for q_ in range(8):
                        b_ = half * 8 + q_
                        ins = e.transpose(out=pvw[0:64, q_, :], in_=Ks.ap[:, b_, j * 64:(j + 1) * 64],
                                          identity=ident.ap[:])
                    return ins
                S.op("pe", f, [Ks, ident], [PS[2 + half]])
                cp(KTs.ap[0:64, half * 1024:(half + 1) * 1024], psb(2 + half)[0:64, :], [PS[2 + half]], [KTs])
            for n_ in range(4):
                pb = PS[4 + n_]
                mm(pb.ap[0:64, :], [(qs.ap[:].rearrange("p h b -> p (h b)")[:, 64 * j:64 * j + 64], KTs.ap[0:64, n_ * 512:(n_ + 1) * 512])], [qs, KTs], [pb])
                tt(smk.ap[0:64, 2 * n_:2 * n_ + 2, :].rearrange("p a (b k) -> p (a b) k", b=2),
                   pb.ap[0:64, :].rearrange("p (b k) -> p b k", b=4),
                   dm.ap[:, 4 * n_:4 * n_ + 4].unsqueeze(2).broadcast_to([64, 4, 128]), ALU.mult, [pb, dm],
                   [smkk(smk, n_)])
            smk_k = [smkk(smk, b4) for b4 in range(4)]
            S.op("dve", lambda e: e.tensor_reduce(
                out=s_s.ap[:], in_=smk.ap[0:64, :, :].rearrange("p a (b k) -> p k (a b)", b=2), axis=AX.X,
                op=ALU.add), smk_k, [s_s])
            m = st1
            S.op("dve", lambda e: e.tensor_reduce(out=m["mx"].ap[:], in_=s_s.ap[:], axis=AX.X, op=ALU.max), [s_s],
                 [m["mx"]])
            tt(m["mx"].ap[:], m["mx"].ap[:], sinkc.ap[:, j:j + 1], ALU.max, [m["mx"], sinkc], [m["mx"]])
            ts(m["nmx"].ap[:], m["mx"].ap[:], -1.0, None, ALU.mult, None, [m["mx"]], [m["nmx"]])
            act(p_s.ap[:], s_s.ap[:], AF.Exp, [s_s, m["nmx"]], [p_s, m["sum"]], bias=m["nmx"].ap[:, 0:1],
                accum_out=m["sum"].ap[:, 0:1])
            tt(m["es"].ap[:], sinkc.ap[:, j:j + 1], m["mx"].ap[:], ALU.subtract, [sinkc, m["mx"]], [m["es"]])
            act(m["es"].ap[:], m["es"].ap[:], AF.Exp, [m["es"]], [m["es"]])
            tt(m["es"].ap[:], m["es"].ap[:], m["sum"].ap[:], ALU.add, [m["es"], m["sum"]], [m["es"]])
            S.op("dve", lambda e: e.reciprocal(out=m["rinv"].ap[:], in_=m["es"].ap[:]), [m["es"]], [m["rinv"]])
            pvw = psb(0)
            S.op("pe", lambda e: e.transpose(out=pvw[:, 0:64], in_=p_s.ap[:], identity=ident.ap[0:64, 0:64]),
                 [p_s, ident], [PS[0]])
            cp(pT_s.ap[:], pvw[:, 0:64], [PS[0]], [pT_s])
            for n_ in range(2):
                pb = PS[2 + n_]
                mm(pb.ap[0:64, :], [(pT_s.ap[:], Vsj[j].ap[:, 8 * n_:8 * n_ + 8, :].rearrange("p b d -> p (b d)"))], [pT_s, Vsj[j]], [pb])
                tt(smk.ap[0:64, 2 * n_:2 * n_ + 2, :].rearrange("p a (b k) -> p (a b) k", b=4),
                   pb.ap[0:64, :].rearrange("p (b k) -> p b k", b=8),
                   dm.ap[:, 8 * n_:8 * n_ + 8].unsqueeze(2).broadcast_to([64, 8, 64]), ALU.mult, [pb, dm],
                   [smkk(smk, n_)])
            S.op("dve", lambda e: e.tensor_reduce(
                out=o_s.ap[:], in_=smk.ap[0:64, 0:4, :].rearrange("p a (b k) -> p k (a b)", b=4), axis=AX.X,
                op=ALU.add), [smkk(smk, 0), smkk(smk, 1)], [o_s])
            ts(o_sb.ap[:], o_s.ap[:], m["rinv"].ap[:, 0:1], None, ALU.mult, None, [o_s, m["rinv"]], [o_sb])
            S.op("pe", lambda e: e.transpose(out=pvw[0:64, 128:192], in_=o_sb.ap[:], identity=ident.ap[0:64, 0:64]),
                 [o_sb, ident], [PS[0]])
            cp(oTs.ap[:, 4 * j:4 * j + 4, :].rearrange("p g b -> p (g b)"), pvw[0:64, 128:192], [PS[0]], [oTs])

    def kv_tail():
        wkv = arena(0, 8 * 256, "p (k n) -> p k n", k=8)
        winv = w["win"].rearrange("(k p) n -> p k n", p=128)
        S.dma("pool", [(wkv.ap[:], winv[:, :, 512:768])], [], [wkv], "wkv")
        tc0 = TS - 128
        for j in range(2):
            pk = PS[2]
            mm(pk.ap[0:64, 0:128], [(wkv.ap[:, k, j * 64:(j + 1) * 64], hT_t[:, k, tc0:tc0 + 128]) for k in range(8)],
               [wkv, hT], [pk])
            cp(tmpD[1].ap[0:64, 128 + j * 128:256 + j * 128], pk.ap[0:64, 0:128], [pk], [tmpD[1]])
        pv_ = PS[3]
        mm(pv_.ap[:, 0:128], [(hT_t[:, k, tc0:tc0 + 128], wkv.ap[:, k, 128:256]) for k in range(8)], [wkv, hT], [pv_])
        cp(tmpD[1].ap[:, 0:128], pv_.ap[:, 0:128], [pv_], [tmpD[1]])

    def whg_buf():
        return arena(0, 8 * 2048, "p (k n) -> p k n", k=8)

    def whg_load(with_f=True):
        wv = w["win"].rearrange("(k p) n -> p k n", p=128)
        if with_f:
            S.dma("pool", [(whg_buf().ap[:], wv[:, :, 768:2816])], [], [whg_buf()], "whg")
        else:
            S.dma("pool", [(whg_buf().ap[:, :, 0:512], wv[:, :, 768:1280]),
                           (whg_buf().ap[:, :, 1024:2048], wv[:, :, 1792:2816])], [], [whg_buf()], "whg")

    def wov_buf():
        return arena(26624, 8192, "p (k n) -> p k n", k=8)

    def wov_load():
        S.dma("pool", [(wov_buf().ap[:], w["wout"].rearrange("(k p) n -> p k n", p=128))], [], [wov_buf()], "wov")

    def hgrn(seg, state_only=False):
        tl = tiles_of(seg)
        S.phase = "%s%d:hgrn" % ("P" if PRE[0] else "M", seg)
        winv = w["win"].rearrange("(k p) n -> p k n", p=128)
        if state_only:
            whg, fo = wpre, 0
        else:
            whg, fo = whg_buf(), 512
            if seg >= 1:
                wov_load()
        io = fo + 512
        pend = [None]
        pm_blk = AR_t[:, THG:THG + 4096].bitcast(F32)
        kk_blk = AR_t[:, THG + 6144:THG + 10240]
        k2_blk = kd2c0.ap[:].rearrange("p h n -> p (h n)")
        pt_blk = Pt.ap[:].rearrange("p h t -> p (h t)")
        kkeys = Pm + keT + kdT + [("kd2c0", h_) for h_ in range(4)] + [("Pt", h_) for h_ in range(4)]
        if not state_only:
            S.dma("sp", [(pm_blk, spP[seg]), (kk_blk, spK[seg]), (k2_blk, spK2[seg]), (pt_blk, spPt[seg])],
                  [("spk", seg)], kkeys, "ld_spk")
        last_seg = (seg == NSEG - 1) and not state_only
        c0, c1 = 0, TS
        for h in range(4):
            pf, pq = PS[2 * (h % 2)], PS[2 * (h % 2) + 1]
            if state_only:
                mm(pf.ap[:, :], [(whg.ap[:, k, fo + h * 128:fo + (h + 1) * 128], hT_t[:, k, c0:c1]) for k in range(8)],
                   [whg, hT], [pf])
            if not state_only:
                mm(pq.ap[:, :], [(whg.ap[:, k, h * 128:(h + 1) * 128], hT_t[:, k, c0:c1]) for k in range(8)],
                   [whg, hT], [pq])
            tf, kh_, fv = tmpA[h % 2], tmpB[h % 2], tmpC[h % 2]
            if state_only:
                act(tf.ap[:], pf.ap[:, :], AF.Tanh, [pf], [tf], scale=0.5)
                ts(kh_.ap[:], tf.ap[:], ccn.ap[:, h:h + 1], cc.ap[:, h:h + 1], ALU.mult, ALU.add, [tf, ccn, cc], [kh_])
                act(fv.ap[:], tf.ap[:], AF.Identity, [tf, cc, cc1], [fv], scale=cc.ap[:, h:h + 1], bias=cc1.ap[:, h:h + 1])
                P = Pm[h]

                if "s" in FEAT:
                    fvv = fv.ap.rearrange("p (c j) -> p c j", j=64)
                    d1v = d1z.ap.rearrange("p (c j) -> p c j", j=64)
                    S.op("dve", lambda e, fvv=fvv, d1v=d1v: e.tensor_copy(out=d1v[:, :, 0:1], in_=fvv[:, :, 0:1]), [fv], [d1z])
                    S.op("dve", lambda e, fvv=fvv: e.memset(fvv[:, :, 0:1], 0.0), [], [fv])
                    S.op("dve", lambda e, fv=fv, P=P: e.tensor_tensor_scan(out=P.ap[:, :], data0=fv.ap[:, :], data1=d1z.ap[:, :],
                                                                            initial=1.0, op0=ALU.mult, op1=ALU.add),
                         [fv, d1z], [P])
                else:
                    def f(e, fv=fv, P=P):
                        for c_ in range(8):
                            ins = e.tensor_tensor_scan(out=P.ap[:, c_ * 64:(c_ + 1) * 64], data0=fv.ap[:, c_ * 64:(c_ + 1) * 64],
                                                       data1=zer.ap[:, :], initial=1.0, op0=ALU.mult, op1=ALU.add)
                        return ins
                    S.op("dve", f, [fv, zer], [P])
                S.op("dve", lambda e, fv=fv, P=P: e.reciprocal(out=fv.ap[:], in_=P.ap[:]), [P], [fv])
                tt(keT[h].ap[:], kh_.ap[:], fv.ap[:], ALU.mult, [kh_, fv], [keT[h]])
                Pv = P.ap.rearrange("p (t c j) -> p t c j", c=2, j=64)
                kev = keT[h].ap.rearrange("p (t c j) -> p t c j", c=2, j=64)
                kdv = kdT[h].ap.rearrange("p (t c j) -> p t c j", c=2, j=64)
                tt(Pt.ap[:, h, :], Pv[:, :, 0, 63], Pv[:, :, 1, 63], ALU.mult, [P], [("Pt", h)])

                def f(e, h=h, kev=kev, kdv=kdv, Pv=Pv):
                    ins = None
                    if "k" in FEAT:
                        ins = e.tensor_tensor(out=kdv, in0=kev, in1=Pv[:, :, :, 63:64].broadcast_to([128, NT, 2, 64]), op=ALU.mult)
                    if "2" in FEAT:
                        ins = e.tensor_tensor(out=kd2c0.ap[:, h, :].rearrange("p (t j) -> p t j", j=64), in0=kev[:, :, 0, :],
                                              in1=Pt.ap[:, h, :].unsqueeze(2).broadcast_to([128, NT, 64]), op=ALU.mult)
                    for t_ in range(NT):
                        if "k" not in FEAT:
                            if not state_only:
                                e.tensor_scalar(out=kdv[:, t_, 0, :], in0=kev[:, t_, 0, :], scalar1=Pv[:, t_, 0, 63:64],
                                                scalar2=None, op0=ALU.mult)
                            ins = e.tensor_scalar(out=kdv[:, t_, 1, :], in0=kev[:, t_, 1, :], scalar1=Pv[:, t_, 1, 63:64],
                                                  scalar2=None, op0=ALU.mult)
                        if "2" not in FEAT:
                            ins = e.tensor_scalar(out=kd2c0.ap[:, h, t_ * 64:(t_ + 1) * 64], in0=kev[:, t_, 0, :],
                                                  scalar1=Pt.ap[:, h, t_:t_ + 1], scalar2=None, op0=ALU.mult)
                    return ins
                S.op("dve", f, [keT[h], P, ("Pt", h)], [kdT[h], ("kd2c0", h)])
            P = Pm[h]
            Pv = P.ap.rearrange("p (t c j) -> p t c j", c=2, j=64)
            if state_only:
                continue
            act(tf.ap[:], pq.ap[:, :], AF.Tanh, [pq], [tf], scale=0.5)
            stt(kh_.ap[:], tf.ap[:], 1.0, pq.ap[:, :], ALU.add, ALU.mult, [tf, pq], [kh_])
            stt(qeT[h].ap[:], kh_.ap[:], 0.5, P.ap[:], ALU.mult, ALU.mult, [kh_, P], [qeT[h]])
            qev = qeT[h].ap.rearrange("p (t c j) -> p t c j", c=2, j=64)

            def f(e, h=h, qev=qev, Pv=Pv):
                if "q" in FEAT:
                    return e.tensor_tensor(out=qe2c1.ap[:, h, :].rearrange("p (t j) -> p t j", j=64), in0=qev[:, :, 1, :],
                                           in1=Pv[:, :, 0, 63:64].broadcast_to([128, NT, 64]), op=ALU.mult)
                for t_ in range(NT):
                    ins = e.tensor_scalar(out=qe2c1.ap[:, h, t_ * 64:(t_ + 1) * 64], in0=qev[:, t_, 1, :],
                                          scalar1=Pv[:, t_, 0, 63:64], scalar2=None, op0=ALU.mult)
                return ins
            S.op("dve", f, [qeT[h], P], [("qe2c1", h)])
        if state_only:
            S.dma("sp", [(spP[seg], pm_blk), (spK[seg], kk_blk), (spK2[seg], k2_blk), (spPt[seg], pt_blk)],
                  kkeys, [("spk", seg)], "st_spk")
        for (i, tp, tc0) in tl:
            if tp != 128:
                continue
            off = tc0 - c0
            pvw = psb(2).rearrange("p (b n) -> p b n", b=8)

            def f(e, off=off, pvw=pvw, i=i):
                for h in range(4):
                    e.transpose(out=pvw[0:64, h, :], in_=kd2c0.ap[:, h, i * 64:(i + 1) * 64], identity=ident.ap[:])
                    ins = e.transpose(out=pvw[64:128, h, :], in_=kdT[h].ap[:, off + 64:off + 128], identity=ident.ap[:])
                return ins
            S.op("pe", f, kdT + [("kd2c0", h) for h in range(4)] + [ident], [PS[2]])
            S.op("act", lambda e, pvw=pvw: e.copy(out=kdtok.ap[:], in_=pvw[:, 0:4, :]), [PS[2]], [kdtok])
            pi_, pg_, pu_ = PS[5], PS[6], PS[0]
            mm(pi_.ap[:, :], [(hT_t[:, k, tc0:tc0 + 128], whg.ap[:, k, io:io + 512]) for k in range(8)], [whg, hT], [pi_])
            S.op("act", lambda e: e.copy(out=vh.ap[:], in_=pi_.ap[:, :]), [pi_], [vh])

            def f(e):
                for h in range(4):
                    ins = e.matmul(pu_.ap[:, h * 128:(h + 1) * 128], lhsT=kdtok.ap[:, h, :], rhs=vh.ap[:, h * 128:(h + 1) * 128],
                                   start=True, stop=True)
                return ins
            S.op("pe", f, [kdtok, vh], [pu_])
            if not state_only:
                sb_old, sb_new = SB[par[0]], SB[1 - par[0]]
                mm(pg_.ap[:, :], [(hT_t[:, k, tc0:tc0 + 128], whg.ap[:, k, 1536:2048]) for k in range(8)], [whg, hT], [pg_])
                tg = tmpD[0]
                act(tg.ap[:, 0:512], pg_.ap[:, :], AF.Tanh, [pg_], [tg], scale=0.5)
                stt(sgg.ap[:], tg.ap[:, 0:512], 1.0, pg_.ap[:, :], ALU.add, ALU.mult, [tg, pg_], [sgg])
                pa, pa2 = PS[7], PS[1]

                def f(e, off=off):
                    for h in range(4):
                        e.matmul(pa.ap[:, h * 128:(h + 1) * 128], lhsT=keT[h].ap[:, off:off + 128],
                                 rhs=qeT[h].ap[:, off:off + 128], start=True, stop=True)
                    for h in range(4):
                        ins = e.matmul(pa2.ap[0:64, h * 64:(h + 1) * 64], lhsT=kdT[h].ap[:, off:off + 64],
                                       rhs=qeT[h].ap[:, off + 64:off + 128], start=True, stop=True)
                    return ins
                S.op("pe", f, keT + qeT + kdT, [pa, pa2])
                if pend[0] is not None:
                    pend[0]()
                    pend[0] = None
                tt(ATs.ap[:], pa.ap[:, :].rearrange("p (h t) -> p h t", h=4),
                   tri.ap[:].unsqueeze(1).broadcast_to([128, 4, 128]), ALU.mult, [pa, tri], [ATs])
                cp(ATs.ap[0:64, :, 64:128], pa2.ap[0:64, 0:256].rearrange("p (h t) -> p h t", h=4), [pa2, ATs], [ATs])
                po = PS[3 + (i % 2)]

                def f(e, off=off, i=i, po=po, sb_old=sb_old):
                    for h in range(4):
                        hc = slice(h * 128, (h + 1) * 128)
                        e.matmul(po.ap[:, hc], lhsT=ATs.ap[:, h, :], rhs=vh.ap[:, hc], start=True, stop=False)
                        e.matmul(po.ap[0:64, hc], lhsT=qeT[h].ap[:, off:off + 64], rhs=sb_old.ap[:, h, :],
                                 start=False, stop=True, skip_group_check=True)
                        ins = e.matmul(po.ap[64:128, hc], lhsT=qe2c1.ap[:, h, i * 64:(i + 1) * 64], rhs=sb_old.ap[:, h, :],
                                       start=False, stop=True, skip_group_check=True)
                    return ins
                S.op("pe", f, [ATs, vh, sb_old] + qeT + [("qe2c1", h) for h in range(4)], [po])
            for h in range(4):
                stt(Sf.ap[:, h, :], Sf.ap[:, h, :], Pt.ap[:, h, i:i + 1], pu_.ap[:, h * 128:(h + 1) * 128], ALU.mult, ALU.add,
                    [("Sf", h), ("Pt", h), pu_], [("Sf", h)])
            if not state_only:
                S.op("act", lambda e, sb_new=sb_new: e.copy(out=sb_new.ap[:], in_=Sf.ap[:]),
                     [("Sf", h) for h in range(4)], [sb_new])
                par[0] ^= 1
                pend[0] = hgrn_epilogue(po, [po], 128, tc0, defer=True)
        if pend[0] is not None:
            pend[0]()
            pend[0] = None
        if last_seg:
            S.dma("sp", [(nhp.rearrange("h k v -> k h v"), Sf.ap[:])], [("Sf", h) for h in range(4)], [], "o_nhp")
        if seg == 0 and not state_only:
            sample_hgrn(whg)

    def hgrn_epilogue(po, pokeys, tp, tc0, defer=False):
        for h in range(4):
            act(junk.ap[0:tp, 0:128], po.ap[0:tp, h * 128:(h + 1) * 128], AF.Square, pokeys, [junk, ("ssh", h)],
                accum_out=ssh.ap[0:tp, h:h + 1])
        rsqrt(rsh.ap[0:tp, :], ssh.ap[0:tp, :], 1.0 / 128, EPS, [("ssh", h) for h in range(4)], [rsh],
              rsq_tmp[0:tp, 0:4])
        t_ = tmpD[1]
        for h in range(4):
            stt(t_.ap[0:tp, h * 128:(h + 1) * 128], po.ap[0:tp, h * 128:(h + 1) * 128], rsh.ap[0:tp, h:h + 1],
                hgnb.ap[0:tp, :], ALU.mult, ALU.mult, pokeys + [rsh, hgnb], [t_])
        tt(ohn.ap[0:tp, :], t_.ap[0:tp, 0:512], sgg.ap[0:tp, :], ALU.mult, [sgg, t_], [ohn])
        pvw = psb(2).rearrange("p (b n) -> p b n", b=8)

        def part2():
            def f(e):
                for c_ in range(4):
                    ins = e.transpose(out=pvw[:, c_, 0:tp], in_=ohn.ap[0:tp, c_ * 128:(c_ + 1) * 128],
                                      identity=ident.ap[0:tp, 0:tp])
                return ins
            S.op("pe", f, [ohn, ident], [PS[2]])
            cp(ohT_t[:, :, tc0:tc0 + tp], pvw[:, 0:4, 0:tp], [PS[2]], [ohT])
        if defer:
            return part2
        part2()

    def sample_hgrn(whg):
        sc0 = TS
        S.phase = "M0:shgrn"
        B = NSAMP
        for h in range(4):
            pf = PS[h % 2]
            mm(pf.ap[:, 0:B], [(whg.ap[:, k, 512 + h * 128:512 + (h + 1) * 128], hT_t[:, k, sc0:sc0 + B]) for k in range(8)],
               [whg, hT], [pf])
            mm(pf.ap[:, 64:64 + B], [(whg.ap[:, k, h * 128:(h + 1) * 128], hT_t[:, k, sc0:sc0 + B]) for k in range(8)],
               [whg, hT], [pf])
            tf = tmpA[h % 2]
            act(tf.ap[:, 0:B], pf.ap[:, 0:B], AF.Tanh, [pf], [tf], scale=0.5)
            act(fvs.ap[:, h, :], tf.ap[:, 0:B], AF.Identity, [tf, cc, cc1], [fvs], scale=cc.ap[:, h:h + 1],
                bias=cc1.ap[:, h:h + 1])
            act(tf.ap[:, 64:64 + B], pf.ap[:, 64:64 + B], AF.Tanh, [pf], [tf], scale=0.5)
            stt(qss.ap[:, h, :], tf.ap[:, 64:64 + B], 1.0, pf.ap[:, 64:64 + B], ALU.add, ALU.mult, [tf, pf], [qss])
        ts(qss.ap[:], qss.ap[:], 0.5, None, ALU.mult, None, [qss], [qss])
        pf, pi_, pg_ = PS[2], PS[3], PS[4]
        mm(pf.ap[0:B, :], [(hT_t[:, k, sc0:sc0 + B], whg.ap[:, k, 512:1024]) for k in range(8)], [whg, hT], [pf])
        mm(pi_.ap[0:B, :], [(hT_t[:, k, sc0:sc0 + B], whg.ap[:, k, 1024:1536]) for k in range(8)], [whg, hT], [pi_])
        mm(pg_.ap[0:B, :], [(hT_t[:, k, sc0:sc0 + B], whg.ap[:, k, 1536:2048]) for k in range(8)], [whg, hT], [pg_])
        tf = tmpA[0]
        act(tf.ap[0:B, :], pf.ap[0:B, :], AF.Tanh, [pf], [tf], scale=0.5)
        tB = tmpB[0]
        tt(tB.ap[0:B, :], tf.ap[0:B, :], ccrn.ap[:], ALU.mult, [tf, ccrn], [tB])
        tt(khtok.ap[:], tB.ap[0:B, :], ccr.ap[:], ALU.add, [tB, ccr], [khtok])
        tg = tmpD[0]
        act(tg.ap[0:B, 0:512], pg_.ap[0:B, :], AF.Tanh, [pg_], [tg], scale=0.5)
        stt(sgg.ap[0:B, :], tg.ap[0:B, 0:512], 1.0, pg_.ap[0:B, :], ALU.add, ALU.mult, [tg, pg_], [sgg])
        for n_ in range(4):
            Sg = (Ssg, Ssg2)[n_ % 2]
            S.dma("sp", [(Sg.ap[:, h, :, :], st[4 * n_:4 * n_ + 4, h].rearrange("b k v -> k b v")) for h in range(4)], [], [Sg], "ld_st%d" % (n_ % 2))
            Ss4 = Sg.ap
            for h in range(4):
                ve = vexp[h]
                if n_ == 0:
                    tt(ve.ap[0:B, :, :], pi_.ap[0:B, h * 128:(h + 1) * 128].unsqueeze(1).broadcast_to([B, 16, 128]),
                       id16.ap[:].unsqueeze(2).broadcast_to([B, 16, 128]), ALU.mult, [pi_, id16], [ve])
                pu_ = PS[5 + (h % 2)]
                mm(pu_.ap[:, :], [(khtok.ap[0:B, h * 128:(h + 1) * 128], ve.ap[0:B, 4 * n_:4 * n_ + 4, :].rearrange("p b v -> p (b v)"))],
                   [khtok, ve], [pu_])
                for bb in range(4):
                    b_ = 4 * n_ + bb
                    stt(Ss4[:, h, bb, :], Ss4[:, h, bb, :], fvs.ap[:, h, b_:b_ + 1],
                        pu_.ap[:, bb * 128:(bb + 1) * 128], ALU.mult, ALU.add, [Sg, fvs, pu_], [Sg])
            S.dma("sp", [(nhs[4 * n_:4 * n_ + 4, h].rearrange("b k v -> k b v"), Sg.ap[:, h, :, :]) for h in range(4)], [Sg], [], "o_nhs%d" % (n_ % 2))
            for h in range(4):
                pb = PS[h % 2]
                mm(pb.ap[0:B, :], [(qss.ap[:, h, :], Ss4[:, h, :, :].rearrange("p b v -> p (b v)"))], [qss, Sg], [pb])
                tt(smk2.ap[0:B, 0:2, :].rearrange("p a (b k) -> p (a b) k", b=2),
                   pb.ap[0:B, :].rearrange("p (b k) -> p b k", b=4),
                   id16.ap[:, 4 * n_:4 * n_ + 4].unsqueeze(2).broadcast_to([B, 4, 128]), ALU.mult, [pb, id16],
                   [smkk(smk2, 0)])
                S.op("dve", lambda e, h=h, n_=n_: e.tensor_reduce(
                    out=smk2.ap[0:B, 4 + (n_ % 2) * 2 + h // 2, (h % 2) * 128:(h % 2) * 128 + 128],
                    in_=smk2.ap[0:B, 0:2, :].rearrange("p a (b k) -> p k (a b)", b=2), axis=AX.X, op=ALU.add),
                    [smkk(smk2, 0)], [smkk(smk2, 2 + (n_ % 2))])
            src = smk2.ap[0:B, 4 + (n_ % 2) * 2:6 + (n_ % 2) * 2, :].rearrange("p a k -> p (a k)")
            if n_ == 0:
                cp(o_hs.ap[:], src, [smkk(smk2, 2 + (n_ % 2))], [o_hs])
            else:
                tt(o_hs.ap[:], o_hs.ap[:], src, ALU.add, [smkk(smk2, 2 + (n_ % 2)), o_hs], [o_hs])
        hgrn_epilogue(o_hs, [o_hs], B, TS)

    def mixout(seg):
        tl = tiles_of(seg)
        S.phase = "M%d:mixout" % seg
        bl = blocks_of(seg)
        winv = w["win"].rearrange("(k p) n -> p k n", p=128)
        wov = wov_buf()
        mT = mT0 if seg == 0 else mT1
        MC = NC if seg == 0 else TS
        wao64 = arena(10752, 8192, "p (h n) -> p h n", h=8)
        if seg == 0:
            wov_load()
            S.dma("pool", [(wao64.ap[0:64, :, :], w["wao"].rearrange("(h p) n -> p h n", p=64))], [], [wao64], "wao64")
        slots = [(arena(0, 4096, "p (k a n) -> p k a n", k=8, a=2), arena(4096, 1024, "p (c n) -> p c n", c=4),
                  arena(5120, 1024, "p (c n) -> p c n", c=4)),
                 (arena(22528, 4096, "p (k a n) -> p k a n", k=8, a=2), arena(34816, 1024, "p (c n) -> p c n", c=4),
                  arena(35840, 1024, "p (c n) -> p c n", c=4))]

        def load(m_):
            a, b, c = slots[m_ % 2]
            cs = slice(m_ * 256, (m_ + 1) * 256)
            S.dma("pool", [(a.ap[:, :, 0, :], winv[:, :, 2816 + m_ * 256:2816 + (m_ + 1) * 256]),
                           (a.ap[:, :, 1, :], winv[:, :, 3840 + m_ * 256:3840 + (m_ + 1) * 256]),
                           (b.ap[:], w["wao"].rearrange("(c p) n -> p c n", p=128)[:, :, cs]),
                           (c.ap[:], w["who"].rearrange("(c p) n -> p c n", p=128)[:, :, cs])],
                  [], [a, b, c], "wm%d" % (m_ % 2))
        load(0)
        for n_ in range(8):
            m_, nn = n_ // 2, n_ % 2
            ns = slice(nn * 128, (nn + 1) * 128)
            if nn == 0 and m_ + 1 < 4:
                load(m_ + 1)
            if n_ == 5 and seg >= 1 and stage >= 3:
                ffn_load(w["w2g"], w["w2u"], w["w2d"], 0, 0)
            a, b, c = slots[m_ % 2]
            for (c0, c1) in bl:
                N = c1 - c0
                pb0 = 4 * (n_ % 2)
                pga, pgh, ppa, pph = PS[pb0], PS[pb0 + 1], PS[pb0 + 2], PS[pb0 + 3]
                mm(pga.ap[:, 0:N], [(a.ap[:, k, 0, ns], hT_t[:, k, c0:c1]) for k in range(8)], [a, hT], [pga])
                mm(pgh.ap[:, 0:N], [(a.ap[:, k, 1, ns], hT_t[:, k, c0:c1]) for k in range(8)], [a, hT], [pgh])
                if c0 < TS:
                    mm(ppa.ap[:, 0:N], [(b.ap[:, c_, ns], oaT_t[:, c_, c0:c1]) for c_ in range(4)], [b, oaT], [ppa])
                else:
                    mm(ppa.ap[:, 0:N], [(wao64.ap[0:64, h, n_ * 128:(n_ + 1) * 128], oTs.ap[:, h, :]) for h in range(8)],
                       [wao64, oTs], [ppa])
                mm(pph.ap[:, 0:N], [(c.ap[:, c_, ns], ohT_t[:, c_, c0:c1]) for c_ in range(4)], [c, ohT], [pph])
                ta, th = tmpA[n_ % 2], tmpB[n_ % 2]
                act(ta.ap[:, 0:N], pga.ap[:, 0:N], AF.Tanh, [pga], [ta], scale=0.5)
                act(th.ap[:, 0:N], pgh.ap[:, 0:N], AF.Tanh, [pgh], [th], scale=0.5)
                m1, m2 = (tmpC[0], tmpC[1]) if n_ % 2 == 0 else (tmpD[0], tmpD[1])
                stt(m1.ap[:, 0:N], ta.ap[:, 0:N], 1.0, ppa.ap[:, 0:N], ALU.add, ALU.mult, [ta, ppa], [m1])
                stt(m2.ap[:, 0:N], th.ap[:, 0:N], 1.0, pph.ap[:, 0:N], ALU.add, ALU.mult, [th, pph], [m2])
                tt(mT.ap[:, n_, c0:c1], m1.ap[:, 0:N], m2.ap[:, 0:N], ALU.add, [m1, m2], [akeys(mT.off + n_ * MC, MC)], eng="pool")
        mkeys = [mT]
        gp = gpost["nmpost"]
        for (i, tp, tc0) in tl:
            for half in range(2):
                py = PS[4 + (2 * i + half) % 4]
                mm(py.ap[0:tp, :], [(mT.ap[:, n_, tc0:tc0 + tp], wov.ap[:, n_, half * 512:(half + 1) * 512])
                                    for n_ in range(8)], mkeys + [wov], [py])
                act(junk.ap[0:tp, 0:512], py.ap[0:tp, :], AF.Square, [py, ss], [junk, ("ss2", half)],
                    accum_out=ss.ap[0:tp, 12 + half:13 + half])
            tt(ss.ap[0:tp, 14:15], ss.ap[0:tp, 12:13], ss.ap[0:tp, 13:14], ALU.add, [("ss2", 0), ("ss2", 1)], [("ss2", 2)])
            rsqrt(rstd.ap[0:tp, 15:16], ss.ap[0:tp, 14:15], 1.0 / D, 4.0 * EPS, [("ss2", 2)], [("rstd2",)],
                  rsq_tmp[0:tp, 0:1])
            for half in range(2):
                py = PS[4 + (2 * i + half) % 4]
                t_ = tmpD[half]
                stt(t_.ap[0:tp, 0:512], py.ap[0:tp, :], rstd.ap[0:tp, 15:16], gp.ap[0:tp, half * 512:(half + 1) * 512],
                    ALU.mult, ALU.mult, [py, ("rstd2",), gp], [t_])
                tt(R_t[0:tp, i, half * 512:(half + 1) * 512], R_t[0:tp, i, half * 512:(half + 1) * 512],
                   t_.ap[0:tp, 0:512], ALU.add, [R[i], t_], [R[i]], eng="pool")
        if seg == 0 and stage >= 3:
            ffn_load(w["w2g"], w["w2u"], w["w2d"], 0, 0)

    def derived_constants():
        for k in ("n1post", "n2post"):
            ts(gpost[k].ap[:], gpost[k].ap[:], 0.5, None, ALU.mult, None, [gpost[k]], [gpost[k]])
        ts(hgnb.ap[:], hgnb.ap[:], 0.5, None, ALU.mult, None, [hgnb], [hgnb])
        tt(cc.ap[:], lbt.ap[:, 0, :], lbt.ap[:, 1, :], ALU.subtract, [lbt], [cc])
        act(cc.ap[:], cc.ap[:], AF.Tanh, [cc], [cc], scale=0.5)
        ts(ccn.ap[:], cc.ap[:], 0.25, -0.25, ALU.mult, ALU.add, [cc], [ccn])
        ts(cc.ap[:], cc.ap[:], -0.25, 0.25, ALU.mult, ALU.add, [cc], [cc])
        ts(cc1.ap[:], cc.ap[:], -1.0, 1.0, ALU.mult, ALU.add, [cc], [cc1])
        tt(ccr.ap[:], lbr.ap[:, 0, :], lbr.ap[:, 1, :], ALU.subtract, [lbr], [ccr])
        act(ccr.ap[:], ccr.ap[:], AF.Tanh, [ccr], [ccr], scale=0.5)
        ts(ccrn.ap[:], ccr.ap[:], 0.25, -0.25, ALU.mult, ALU.add, [ccr], [ccrn])
        ts(ccr.ap[:], ccr.ap[:], -0.25, 0.25, ALU.mult, ALU.add, [ccr], [ccr])

    selc = Buf(sb("selc", [128, 1]), ["selc"])
    S.dma("sp", [(selc.ap[:], vec[10:11, 0:1].broadcast_to([128, 1]))], [], [selc], "csel")
    PRE[0] = True
    winv_ = w["win"].rearrange("(k p) n -> p k n", p=128)
    S.dma("pool", [(wpre.ap[:, :, :], winv_[:, :, 1280:2304])], [], [wpre], "wpre")

    def x1rows(seg, i, tp, c0):
        return x1s[seg * TS + c0: seg * TS + c0 + 128, :] if tp == 128 else x1s[NSEG * TS: NSEG * TS + NSAMP, :]

    for pseg in range(NSEG):
        for (i, tp, c0) in tiles_of(pseg):
            if tp == 128:
                if pseg > 0:
                    S.dma("sp", [(R[i].ap[:, :], xp[pseg * TS + c0: pseg * TS + c0 + 128, :])], [], [R[i]], "ldx%d" % (i % 4))
        ffn(pseg, w["w1g"], w["w1u"], w["w1d"], "n1pre", "n1post", final=False, s0=1, preloaded=True,
            after_norm=(derived_constants if pseg == 0 else None))
        for (i, tp, c0) in tiles_of(pseg):
            S.dma("sp", [(x1rows(pseg, i, tp, c0), R[i].ap[0:tp, :])], [R[i]], [("x1s", pseg, i)], "spill%d" % (i % 5))
        norm_to_hT(pseg, "nmpre")
        S.dma("sp", [(spH[pseg], hT_t[:].rearrange("p k n -> p (k n)"))], [hT], [("sph", pseg)], "st_sph")
        if pseg == NSEG - 1:
            kv_tail()
        else:
            ffn_load(w["w1g"], w["w1u"], w["w1d"], 0, 1)
        hgrn(pseg, state_only=True)
    PRE[0] = False
    S.phase = "exchange"
    S.dma("pool", [(wpre.ap[:, :, 0:768], winv_[:, :, 0:768])], [], [wpre], "wpre")
    S.dma("pool", [(xsend[:, 0:512], Sf.ap[:].rearrange("p h v -> p (h v)")),
                   (xsend[:, 512:896], tmpD[1].ap[:, 0:384])], [("Sf", h) for h in range(4)] + [tmpD[1]], ["xsend"], "xsend")
    S.collective("AllGather", [[0, 1], [2, 3], [4, 5], [6, 7]], xsend.opt(), xrecv.opt(), ["xsend"], ["xrecv"])
    S.dma("pool", [(tmpD[0].ap[:, 0:896], xrecv[0:128, 0:896])], ["xrecv"], [tmpD[0]], "xrecv")
    def consume_exchange():
        ts(Sf.ap[:].rearrange("p h v -> p (h v)"), tmpD[0].ap[:, 0:512], selc.ap[:, 0:1], None, ALU.mult, None,
           [tmpD[0], selc] + [("Sf", h) for h in range(4)], [("Sf", h) for h in range(4)])
        S.op("act", lambda e: e.copy(out=SB[par[0]].ap[:], in_=Sf.ap[:]), [("Sf", h) for h in range(4)], [SB[par[0]]])
        ts(vtok[0].ap[:], tmpD[0].ap[:, 512:640], selc.ap[:, 0:1], None, ALU.mult, None, [tmpD[0], selc], [vtok[0]])
        ts(kT.ap[:, :, 0:128], tmpD[0].ap[0:64, 640:896].rearrange("p (j n) -> p j n", j=2), selc.ap[0:64, 0:1], None,
           ALU.mult, None, [tmpD[0], selc], [kT])

    for seg in range(NSEG):
        S.phase = "M%d:reload" % seg
        S.dma("sp", [(hT_t[:].rearrange("p k n -> p (k n)"), spH[seg])], [("sph", seg)], [hT], "ld_sph")
        for (i, tp, c0) in tiles_of(seg):
            S.dma("sp", [(R[i].ap[0:tp, :], x1rows(seg, i, tp, c0))], [("x1s", seg, i)], [R[i]], "ldx%d" % (i % 5))
        if seg == 0:
            consume_exchange()
        attention(seg)
        hgrn(seg)
        mixout(seg)
        ffn(seg, w["w2g"], w["w2u"], w["w2d"], "n2pre", "n2post", final=True, s0=0, preloaded=True)
    S.finish()
    es.close()
    if os.environ.get("K_PHASEMAP"):
        import json
        json.dump(S.pmap, open(os.environ["K_PHASEMAP"], "w"))
    return nc


def _consts():
    cm = np.zeros((128, 800), np.float32)
    cm[:, 0:128] = np.eye(128, dtype=np.float32)
    s = np.arange(128)[:, None]
    t = np.arange(128)[None, :]
    cm[:, 128:256] = ((s <= t) & (s // 64 == t // 64)).astype(np.float32)
    i = np.arange(128)[:, None]
    j = np.arange(256)[None, :]
    rel = i + 128 - j
    valid = (rel >= 0) & (rel < 128)
    cm[:, 256:512] = np.where(valid, 0.0, NEG)
    cm[:, 512:768] = np.where(valid & (j >= 128), 0.0, NEG)
    cm[0:16, 768:784] = np.eye(16, dtype=np.float32)
    p = np.arange(64)[:, None]
    cm[0:64, 784:800] = ((p % 16) == np.arange(16)[None, :])
    return cm


_NC_CACHE = {}


def kernel(x_prompt, x_sample, cache_k, cache_v, state_hgrn,
           norm_ffn1_pre, norm_ffn1_post, w_ffn1_gate, w_ffn1_up, w_ffn1_down,
           norm_mix_pre, norm_mix_post, w_in, attn_sinks, hgrn_lb_logits, hgrn_norm,
           w_attn_out, w_hgrn_out, w_out,
           norm_ffn2_pre, norm_ffn2_post, w_ffn2_gate, w_ffn2_up, w_ffn2_down):
    stage = int(os.environ.get("K_STAGE", "99"))
    f = lambda a: np.ascontiguousarray(np.asarray(a, dtype=np.float32))
    vec = np.zeros((11, D), np.float32)
    vec[0] = f(norm_ffn1_pre)[0]
    vec[1] = f(norm_ffn1_post)[0]
    vec[2] = f(norm_mix_pre)[0]
    vec[3] = f(norm_mix_post)[0]
    vec[4] = f(norm_ffn2_pre)[0]
    vec[5] = f(norm_ffn2_post)[0]
    vec[6, 0:8] = f(attn_sinks)[0]
    vec[7, 0:512] = f(hgrn_lb_logits)[0]
    vec[8, 0:512] = f(hgrn_lb_logits)[1]
    vec[9, 0:128] = f(hgrn_norm)[0]
    shared = {
        "w1g": f(w_ffn1_gate)[0], "w1u": f(w_ffn1_up)[0], "w1d": f(w_ffn1_down)[0], "win": f(w_in)[0],
        "wao": f(w_attn_out)[0], "who": f(w_hgrn_out)[0], "wout": f(w_out)[0],
        "w2g": f(w_ffn2_gate)[0], "w2u": f(w_ffn2_up)[0], "w2d": f(w_ffn2_down)[0],
        "vec": vec,
    }
    xpf, xsf = f(x_prompt), f(x_sample)
    ckf, cvf, stf = f(cache_k), f(cache_v), f(state_hgrn)
    in_maps = []
    cm_first = _consts()
    cm_second = cm_first.copy()
    cm_second[:, 512:768] = cm_first[:, 256:512]
    HALF = NSEG * TS
    for c in range(8):
        b, half = c // 2, c % 2
        m = dict(shared)
        m["cmat"] = cm_second if half else cm_first
        vc = vec.copy()
        vc[10, 0] = float(half)
        m["vec"] = vc
        m["xp"] = xpf[b][half * HALF:(half + 1) * HALF]
        m["xs"] = xsf[16 * c:16 * c + 16, 0, :]
        m["ck"] = ckf[0, 16 * c:16 * c + 16].reshape(16, 128, 128)
        m["cv"] = cvf[0, 16 * c:16 * c + 16].reshape(16, 128, 128)
        m["st"] = stf[0, 16 * c:16 * c + 16]
        in_maps.append(m)
    if stage not in _NC_CACHE:
        _NC_CACHE[stage] = build(stage)
    nc = _NC_CACHE[stage]
    in_maps = [{k: v for k, v in m.items() if k in _DECL} for m in in_maps]
    res = run_bass_kernel_spmd(nc, in_maps, core_ids=list(range(8))).results
    y_prompt = np.stack([np.concatenate([res[2 * b]["yp"], res[2 * b + 1]["yp"]], 0) for b in range(4)], 0)
    y_sample = np.concatenate([res[c]["ys"] for c in range(8)], 0)[:, None, :]
    nkp = np.stack([res[2 * b + 1]["nkp"].reshape(128, 2, 64) for b in range(4)], 0)[None]
    nvp = np.stack([res[2 * b + 1]["nvp"].reshape(128, 2, 64) for b in range(4)], 0)[None]
    nhp = np.stack([res[2 * b + 1]["nhp"] for b in range(4)], 0)[None]
    nks = np.concatenate([res[c]["nks"].reshape(16, 128, 2, 64) for c in range(8)], 0)[None]
    nvs = np.concatenate([res[c]["nvs"].reshape(16, 128, 2, 64) for c in range(8)], 0)[None]
    nhs = np.concatenate([res[c]["nhs"] for c in range(8)], 0)[None]
    return (y_prompt.astype(np.float32), y_sample.astype(np.float32), nkp.astype(np.float32),
            nvp.astype(np.float32), nhp.astype(np.float32), nks.astype(np.float32), nvs.astype(np.float32),
            nhs.astype(np.float32))
```
